# Optimizing a Trainium2 kernel written in Bass

```python
import math
import jax, jax.numpy as jnp
from jax import lax
import numpy as np

D_MODEL = 1024
BATCH = 8
SEQ = 2048
DEPTH = 4

N_MIXERS = 2
N_A = (DEPTH + 1) // 2
N_B = DEPTH // 2

CHUNK = 128
A_WIDTH = D_MODEL
A_GROUPS = 8
A_GROUP_DIM = A_WIDTH // A_GROUPS

N_HEADS = 8
HEAD_DIM = D_MODEL // N_HEADS // 2
V_HEAD_DIM = 2 * HEAD_DIM
QK_WIDTH = N_HEADS * 2 * HEAD_DIM
V_WIDTH = N_HEADS * V_HEAD_DIM
Q_BLOCK = 128
ROT_DIM = HEAD_DIM // 4
ROPE_THETA = 500000.0

D_FF = 4 * D_MODEL

PLE_DIM = 256
MAX_POS_OFFSET = 4096

RMS_EPS = 1e-6
LN_EPS = 1e-5
SUBLN_EPS = 1e-5

kernel_name = "hybrid_gmlp_diffattn_trunk"


def _rmsnorm(x, g, eps=RMS_EPS):
    xf = x.astype(jnp.float32)
    y = xf * lax.rsqrt(jnp.mean(xf * xf, axis=-1, keepdims=True) + eps)
    return (y * g.astype(jnp.float32)).astype(x.dtype)


def _layernorm(x, g, b):
    xf = x.astype(jnp.float32)
    mu = jnp.mean(xf, axis=-1, keepdims=True)
    var = jnp.mean(jnp.square(xf - mu), axis=-1, keepdims=True)
    y = (xf - mu) * lax.rsqrt(var + LN_EPS)
    return (y * g.astype(jnp.float32) + b.astype(jnp.float32)).astype(x.dtype)


def _rotary_tables(positions, dtype):
    inv_freq = ROPE_THETA ** (-(jnp.arange(0, ROT_DIM, 2, dtype=jnp.float32) / ROT_DIM))
    ang = positions.astype(jnp.float32)[..., None] * inv_freq
    cos = jnp.cos(ang)[:, :, None, None, :].astype(dtype)
    sin = jnp.sin(ang)[:, :, None, None, :].astype(dtype)
    return cos, sin


def _partial_rotary(x, cos, sin):
    xr, xp = x[..., :ROT_DIM], x[..., ROT_DIM:]
    x1, x2 = jnp.split(xr, 2, axis=-1)
    return jnp.concatenate([x1 * cos - x2 * sin, x2 * cos + x1 * sin, xp], axis=-1)


def _chunked_gmlp(a, w_in, ln_g, ln_b, w_s, b_s, w_out):
    B, S, _ = a.shape
    z = jax.nn.gelu(a @ w_in)
    u, v = jnp.split(z, 2, axis=-1)
    v = _layernorm(v, ln_g, ln_b)
    v = v.reshape(B, S // CHUNK, CHUNK, A_GROUPS, A_GROUP_DIM)
    w_causal = w_s * jnp.tril(jnp.ones((CHUNK, CHUNK), dtype=w_s.dtype))
    s = jnp.einsum('gts,bnsgd->bntgd', w_causal, v) + b_s.T[:, :, None]
    y = u * s.reshape(B, S, A_WIDTH)
    return y @ w_out


def _diff_attention(a, positions, w_qkv, lam_q1, lam_k1, lam_q2, lam_k2, subln_g, w_o, lambda_init):
    B, S, _ = a.shape
    qkv = a @ w_qkv
    q, k, v = jnp.split(qkv, [QK_WIDTH, 2 * QK_WIDTH], axis=-1)
    q = q.reshape(B, S, N_HEADS, 2, HEAD_DIM)
    k = k.reshape(B, S, N_HEADS, 2, HEAD_DIM)
    v = v.reshape(B, S, N_HEADS, V_HEAD_DIM)
    cos, sin = _rotary_tables(positions, q.dtype)
    q = _partial_rotary(q, cos, sin)
    k = _partial_rotary(k, cos, sin)
    f32 = jnp.float32
    lam = (jnp.exp(jnp.sum(lam_q1.astype(f32) * lam_k1.astype(f32)))
           - jnp.exp(jnp.sum(lam_q2.astype(f32) * lam_k2.astype(f32))) + lambda_init)
    scale = HEAD_DIM ** -0.5
    outs = []
    for blk in range(S // Q_BLOCK):
        q0 = blk * Q_BLOCK
        q1 = q0 + Q_BLOCK
        qb = q[:, q0:q1]
        kb = k[:, :q1]
        vb = v[:, :q1]
        sc = jnp.einsum('bqhcd,bkhcd->bhcqk', qb, kb).astype(f32) * scale
        mask = (q0 + jnp.arange(Q_BLOCK))[:, None] >= jnp.arange(q1)[None, :]
        sc = jnp.where(mask, sc, -jnp.inf)
        pr = jax.nn.softmax(sc, axis=-1)
        attn = (pr[:, :, 0] - lam * pr[:, :, 1]).astype(vb.dtype)
        outs.append(jnp.einsum('bhqk,bkhe->bqhe', attn, vb))
    o = jnp.concatenate(outs, axis=1)
    o = _rmsnorm(o, subln_g, SUBLN_EPS) * (1.0 - lambda_init)
    return o.reshape(B, S, V_WIDTH) @ w_o


def _sqrelu_mlp(x, w_up, w_down):
    return jnp.square(jax.nn.relu(x @ w_up)) @ w_down


def setup_inputs(seed: int = 0) -> dict:
    key = jax.random.key(seed)
    ks = jax.random.split(key, 32)
    f32 = jnp.float32

    def nrm(k, shape, scale):
        return jax.random.normal(k, shape, f32) * scale

    def gain(k, shape):
        return 1.0 + 0.05 * jax.random.normal(k, shape, f32)

    x = nrm(ks[0], (BATCH, SEQ, D_MODEL), 1.0)
    p = nrm(ks[1], (DEPTH, BATCH, SEQ, PLE_DIM), 1.0)
    offsets = jax.random.randint(ks[2], (BATCH, 1), 0, MAX_POS_OFFSET, dtype=jnp.int32)
    positions = (offsets + jnp.arange(SEQ, dtype=jnp.int32)[None, :]).astype(jnp.int32)
    return {
        "x": x,
        "p": p,
        "positions": positions,
        "g_mix_pre": gain(ks[3], (DEPTH, D_MODEL)),
        "g_mix_post": gain(ks[4], (DEPTH, D_MODEL)),
        "g_ffn_pre": gain(ks[5], (DEPTH, D_MODEL)),
        "g_ffn_post": gain(ks[6], (DEPTH, D_MODEL)),
        "g_ple_pre": gain(ks[7], (DEPTH, D_MODEL)),
        "g_ple_post": gain(ks[8], (DEPTH, D_MODEL)),
        "a_w_in": nrm(ks[9], (N_A, D_MODEL, 2 * A_WIDTH), D_MODEL ** -0.5),
        "a_ln_g": gain(ks[10], (N_A, A_WIDTH)),
        "a_ln_b": nrm(ks[11], (N_A, A_WIDTH), 0.02),
        "a_w_s": nrm(ks[12], (N_A, A_GROUPS, CHUNK, CHUNK), CHUNK ** -0.5),
        "a_b_s": 1.0 + nrm(ks[13], (N_A, A_GROUPS, CHUNK), 0.1),
        "a_w_out": nrm(ks[14], (N_A, A_WIDTH, D_MODEL), A_WIDTH ** -0.5),
        "b_w_qkv": nrm(ks[15], (N_B, D_MODEL, 2 * QK_WIDTH + V_WIDTH), D_MODEL ** -0.5),
        "b_lam_q1": nrm(ks[16], (N_B, HEAD_DIM), 0.1),
        "b_lam_k1": nrm(ks[17], (N_B, HEAD_DIM), 0.1),
        "b_lam_q2": nrm(ks[18], (N_B, HEAD_DIM), 0.1),
        "b_lam_k2": nrm(ks[19], (N_B, HEAD_DIM), 0.1),
        "b_subln_g": gain(ks[20], (N_B, V_HEAD_DIM)),
        "b_w_o": nrm(ks[21], (N_B, V_WIDTH, D_MODEL), V_WIDTH ** -0.5),
        "f_w_up": nrm(ks[22], (DEPTH, D_MODEL, D_FF), D_MODEL ** -0.5),
        "f_w_down": nrm(ks[23], (DEPTH, D_FF, D_MODEL), D_FF ** -0.5),
        "ple_w_proj": nrm(ks[24], (DEPTH, PLE_DIM, D_MODEL), PLE_DIM ** -0.5),
        "ple_w_gate": nrm(ks[25], (DEPTH, D_MODEL, D_MODEL), D_MODEL ** -0.5),
        "ple_b_gate": nrm(ks[26], (DEPTH, D_MODEL), 0.02),
    }


def reference(x, p, positions, g_mix_pre, g_mix_post, g_ffn_pre, g_ffn_post, g_ple_pre, g_ple_post,
              a_w_in, a_ln_g, a_ln_b, a_w_s, a_b_s, a_w_out,
              b_w_qkv, b_lam_q1, b_lam_k1, b_lam_q2, b_lam_k2, b_subln_g, b_w_o,
              f_w_up, f_w_down, ple_w_proj, ple_w_gate, ple_b_gate):
    h = x
    for i in range(DEPTH):
        j = i // N_MIXERS
        a = _rmsnorm(h, g_mix_pre[i])
        if i % N_MIXERS == 0:
            m = _chunked_gmlp(a, a_w_in[j], a_ln_g[j], a_ln_b[j], a_w_s[j], a_b_s[j], a_w_out[j])
        else:
            lambda_init = 0.8 - 0.6 * math.exp(-0.3 * i)
            m = _diff_attention(a, positions, b_w_qkv[j], b_lam_q1[j], b_lam_k1[j], b_lam_q2[j],
                                b_lam_k2[j], b_subln_g[j], b_w_o[j], lambda_init)
        h = h + _rmsnorm(m, g_mix_post[i])
        f = _sqrelu_mlp(_rmsnorm(h, g_ffn_pre[i]), f_w_up[i], f_w_down[i])
        h = h + _rmsnorm(f, g_ffn_post[i])
        gate = jax.nn.sigmoid(_rmsnorm(h, g_ple_pre[i]) @ ple_w_gate[i] + ple_b_gate[i])
        e = gate * (p[i] @ ple_w_proj[i])
        h = h + _rmsnorm(e, g_ple_post[i])
    return h
```

```python
import math
from contextlib import ExitStack

import numpy as np
import concourse.bass as bass
import concourse.mybir as mybir
from concourse.bass_utils import run_bass_kernel_spmd

F32 = mybir.dt.float32
BF16 = mybir.dt.bfloat16
I32 = mybir.dt.int32
AF = mybir.ActivationFunctionType
ALU = mybir.AluOpType
AX = mybir.AxisListType

D = 1024
S = 2048
T = 512
NT = S // T
DFF = 4096
PLE = 256
DEPTH = 4
NCORES = 8
RMS_EPS = 1e-6
LN_EPS = 1e-5
SUBLN_EPS = 1e-5
NPAR = 258
NCST = 386


class Sem:
    def __init__(self, h, step):
        self.h = h
        self.val = 0
        self.step = step


class Buf:
    def __init__(self, name, arena, lo, hi):
        self.name = name
        self.arena = arena
        self.lo = lo
        self.hi = hi
        self.last_w = None
        self.readers = {}
        self.excl = False


class Prog:
    ENGS = ("pe", "act", "dve", "pool", "sp")

    def __init__(self, nc, stack):
        self.nc = nc
        self.stack = stack
        self.q = {e: [] for e in self.ENGS}
        self.waited = {e: {} for e in self.ENGS}
        self.arenas = {}
        self.nsem = 0
        self.esem = {e: self.sem(1) for e in ("pe", "act", "dve", "pool")}
        self.nb = 0

    def sem(self, step=16):
        self.nsem += 1
        h = self.stack.enter_context(self.nc.semaphore("s%d" % self.nsem))
        return Sem(h, step)

    def buf(self, name, arena=None, lo=0, hi=1):
        if arena is None:
            self.nb += 1
            arena = "_u%d" % self.nb
        b = Buf(name, arena, lo, hi)
        self.arenas.setdefault(arena, []).append(b)
        return b

    def _conf(self, b):
        return [c for c in self.arenas[b.arena] if c.lo < b.hi and b.lo < c.hi]

    def emit(self, eng, fn, reads=(), writes=(), sem=None):
        is_dma = sem is not None
        s = sem if is_dma else self.esem[eng]
        own = self.esem.get(eng)
        need = {}

        def add(dep, kind):
            sm, v = dep
            if (not is_dma) and sm is own:
                if eng == "pe":
                    return
            if need.get(sm, 0) < v:
                need[sm] = v

        excl = [b for b in reads if b.excl]
        if excl:
            reads = tuple(b for b in reads if not b.excl)
            writes = tuple(writes) + tuple(excl)
        for b in reads:
            for c in self._conf(b):
                if c.last_w is not None:
                    add(c.last_w, "raw")
        for b in writes:
            for c in self._conf(b):
                if c.last_w is not None:
                    add(c.last_w, "waw")
                for sm, v in c.readers.items():
                    add((sm, v), "war")
        if is_dma and s.val > 0:
            need[s] = max(need.get(s, 0), s.val)
        waits = []
        wd = self.waited[eng]
        for sm, v in need.items():
            if wd.get(sm, 0) >= v:
                continue
            wd[sm] = v
            waits.append((sm, v))
        s.val += s.step
        self.q[eng].append((waits, fn, s))
        for b in reads:
            b.readers[s] = s.val
        for b in writes:
            b.last_w = (s, s.val)
            b.readers = {}

    def replay(self, e, eng, final_waits=()):
        for waits, fn, s in self.q[eng]:
            for sm, v in waits:
                e.wait_ge(sm.h, v)
            ins = fn(e)
            ins.then_inc(s.h, s.step)
        for sm in final_waits:
            e.wait_ge(sm.h, sm.val)


class _Stop(Exception):
    pass


def build_program(depth=DEPTH, stop=0):
    _stopc = [0]

    def chk():
        _stopc[0] += 1
        if stop and _stopc[0] >= stop:
            raise _Stop()

    nc = bass.Bass("TRN2", target_bir_lowering=False)
    dram = {}

    def din(name, shape, dt=F32):
        dram[name] = nc.dram_tensor(name, list(shape), dt, kind="ExternalInput").ap()
        return dram[name]

    xT = din("xT", [D, S])
    pT = din("pT", [DEPTH, PLE, S])
    pos = din("pos", [1, S], I32)
    par_d = din("par", [128, NPAR])
    cst_d = din("cst", [128, NCST])
    lnrow_d = din("bsrow", [2, 1024])
    wsT_d = din("wsT", [2, 128, 1024])
    lam_d = din("lam", [2, 256])
    a_w_in = din("a_w_in", [2, D, 2 * D])
    a_w_out = din("a_w_out", [2, D, D])
    b_w_qkv = din("b_w_qkv", [2, D, 3 * D])
    b_w_o = din("b_w_o", [2, D, D])
    f_w_up = din("f_w_up", [DEPTH, D, DFF])
    f_w_down = din("f_w_down", [DEPTH, DFF, D])
    ple_w_proj = din("ple_w_proj", [DEPTH, PLE, D])
    ple_w_gate = din("ple_w_gate", [DEPTH, D, D])
    outT = nc.dram_tensor("outT", [D, S], F32, kind="ExternalOutput").ap()

    with ExitStack() as st:
        def sb(name, shape, dt):
            return st.enter_context(nc.sbuf_tensor("sb_" + name, list(shape), dt))

        P = Prog(nc, st)
        h = sb("h", [128, 8, S], F32)
        OV = sb("OV", [128, 16384], BF16)
        AXA = sb("AXA", [128, 8192], BF16)
        aT = sb("aT", [128, 8, T], BF16)
        Qt = sb("Qt", [128, 4, T], BF16)
        Ot = sb("Ot", [128, 4, T], BF16)
        m = sb("m", [128, 8, T], F32)
        sqr = sb("sqr", [128, 4, T], BF16)
        CS = sb("CS", [128, 2, S], BF16)
        Pt = sb("Pt", [128, 8, T], BF16)
        tmp = sb("tmp", [128, 5, T], F32)
        Wr = sb("Wr", [128, 3, 4096], BF16)
        pTb = sb("pTb", [128, 2, T], BF16)
        par = sb("par", [128, NPAR], F32)
        cstf = sb("cstf", [128, NCST], F32)
        cb = sb("cb", [128, 4, 128], BF16)
        sm_ = sb("small", [128, 80], F32)
        ps = st.enter_context(nc.psum_tensor("ps", [128, 8, T], F32))

        OVf = OV[:, 8192:16384].bitcast(F32)
        hid = OV[:, :].rearrange("p (c n) -> p c n", n=T)
        u_v = OV[:, 0:4096].rearrange("p (c n) -> p c n", n=T)
        vh_v = OV[:, 4096:8192].rearrange("p (s n) -> p s n", n=1024)
        vt_v = OVf.rearrange("p (s n) -> p s n", n=1024)
        Kc = OV[:, 0:8192].rearrange("p (h n) -> p h n", n=S)
        Vc = OV[:, 8192:16384].rearrange("p (s n) -> p s n", n=512)
        Oall = AXA[:, :].rearrange("p (h n) -> p h n", n=S)
        E_v = AXA[:, 0:2048].bitcast(F32).rearrange("p (g t) -> p g t", t=128)
        WcT = AXA[:, 2048:3072].rearrange("p (g t) -> p g t", t=128)

        B_h = [[P.buf("h%d_%d" % (c, t)) for t in range(NT)] for c in range(8)]
        B_hid = [P.buf("hid%d" % c, "OV", c * 1024, (c + 1) * 1024) for c in range(32)]
        B_u = [P.buf("u%d" % c, "OV", c * 1024, (c + 1) * 1024) for c in range(8)]
        B_vh = [P.buf("vh%d" % s_, "OV", 8192 + s_ * 2048, 8192 + (s_ + 1) * 2048) for s_ in range(4)]
        B_vt = [P.buf("vt%d" % s_, "OV", 16384 + s_ * 4096, 16384 + (s_ + 1) * 4096) for s_ in range(4)]
        B_K = [[P.buf("K%d_%d" % (hh, t), "OV", hh * 4096 + t * 1024, hh * 4096 + (t + 1) * 1024)
                for t in range(NT)] for hh in range(4)]
        B_V = [P.buf("V%d" % s_, "OV", 16384 + s_ * 1024, 16384 + (s_ + 1) * 1024) for s_ in range(16)]
        B_OV_all = P.buf("OVall", "OV", 0, 32768)
        B_Oall = [[P.buf("Oa%d_%d" % (hh, t), "AX", hh * 4096 + t * 1024, hh * 4096 + (t + 1) * 1024)
                   for t in range(NT)] for hh in range(4)]
        B_E = P.buf("E", "AX", 0, 4096)
        B_WcT = P.buf("WcT", "AX", 4096, 6144)
        B_aT = [P.buf("aT%d" % c) for c in range(8)]
        B_Qt = [P.buf("Qt%d" % c) for c in range(4)]
        B_Ot = [P.buf("Ot%d" % c) for c in range(4)]
        B_m = [P.buf("m%d" % c) for c in range(8)]
        B_sq = [P.buf("sq%d" % c) for c in range(4)]
        B_CS = P.buf("CS")
        B_Pt = [P.buf("Pt%d" % c) for c in range(8)]
        B_tmp = [P.buf("tmp%d" % c) for c in range(5)]
        B_W = [P.buf("W%d" % c) for c in range(3)]
        S_W = [P.sem() for _ in range(3)]
        B_pT = P.buf("pTb")
        S_pT = P.sem()
        B_par = P.buf("par")
        B_cstf = P.buf("cstf")
        B_cb = P.buf("cb")
        B_small = [P.buf("small%d" % i) for i in range(80)]
        B_ps = [P.buf("ps%d" % i) for i in range(8)]
        for b_ in B_ps:
            b_.excl = True
        B_m_all = B_m
        S_misc = [P.sem() for _ in range(4)]
        S_x = [P.sem() for _ in range(8)]
        S_out = [P.sem() for _ in range(8)]

        free_banks = list(range(8))

        def balloc():
            return free_banks.pop(0)

        def bfree(b):
            free_banks.append(b)

        cnt = {"W": 0, "sq": 0, "Pt": 0, "tmp": 0}

        def ring(name, n):
            i = cnt[name] % n
            cnt[name] += 1
            return i

        def pcol(i, kind, c):
            k = (i * 7 + kind) * 8 + c
            return par[:, k:k + 1]

        def wload(src_ap, view_shape=None):
            i = ring("W", 3)
            a, b_ = src_ap.shape[1], src_ap.shape[2]
            dst = Wr[:, i, 0:a * b_].rearrange("p (a b) -> p a b", b=b_)
            P.emit("pool", lambda e, d=dst, s_=src_ap: e.dma_start(out=d, in_=s_),
                   reads=(), writes=(B_W[i],), sem=S_W[i])
            return dst, B_W[i]

        def wslab(w2d, k0, n0, kc=8, ncols=512):
            src = w2d[k0:k0 + kc * 128, n0:n0 + ncols].rearrange("(kc p) n -> p kc n", p=128)
            return wload(src)

        ones_bf = cb[:, 3, :]
        ident_bf = cb[:, 1, :]
        pm_bf = cb[:, 0, :]
        tri_bf = cb[:, 2, :]

        def mm_group(bank, cols, pairs, reads, start=True, stop=True):
            out = ps[:, bank, cols[0]:cols[1]]
            n = len(pairs)

            def fn(e):
                ins = None
                for i, (l, r) in enumerate(pairs):
                    ins = e.matmul(out, l, r, start=(start and i == 0), stop=(stop and i == n - 1))
                return ins
            P.emit("pe", fn, reads=reads, writes=(B_ps[bank],))

        def act(out, in_, func, reads, writes, scale=None, bias=None):
            kw = {}
            if scale is not None:
                kw["scale"] = scale
            if bias is not None:
                kw["bias"] = bias
            P.emit("act", lambda e: e.activation(out=out, in_=in_, func=func, **kw), reads=reads, writes=writes)

        def tt(out, in0, in1, op, reads, writes, eng="dve"):
            P.emit(eng, lambda e: e.tensor_tensor(out=out, in0=in0, in1=in1, op=op), reads=reads, writes=writes)

        def stt(out, in0, scalar, in1, op0, op1, reads, writes):
            P.emit("dve", lambda e: e.scalar_tensor_tensor(out=out, in0=in0, scalar=scalar, in1=in1, op0=op0, op1=op1),
                   reads=reads, writes=writes)

        def ts(out, in0, s1, s2, op0, op1, reads, writes, eng="dve"):
            if op1 is None:
                P.emit(eng, lambda e: e.tensor_scalar(out=out, in0=in0, scalar1=s1, scalar2=None, op0=op0),
                       reads=reads, writes=writes)
            else:
                P.emit(eng, lambda e: e.tensor_scalar(out=out, in0=in0, scalar1=s1, scalar2=s2, op0=op0, op1=op1),
                       reads=reads, writes=writes)

        def tcopy(out, in_, reads, writes, eng="dve"):
            P.emit(eng, lambda e: e.tensor_copy(out=out, in_=in_), reads=reads, writes=writes)

        eps_cols = {}

        def rstd_from_stat(bank, n, scale, eps_col, ti):
            t_ = tmp[:, ti, 0:n]
            act(t_, ps[:, bank, 0:n], AF.Ln, reads=(B_ps[bank], B_small[eps_col]), writes=(B_tmp[ti],),
                scale=scale, bias=sm_[:, eps_col:eps_col + 1])
            act(ps[:, bank, 0:n], t_, AF.Exp, reads=(B_tmp[ti],), writes=(B_ps[bank],), scale=-0.5)

        def stat_accum(bank, n, src_sq, src_buf, first, last):
            mm_group(bank, (0, n), [(ones_bf, src_sq)], reads=(src_buf, B_cb), start=first, stop=last)

        stages = []
        cur = [0]
        nstate = {}

        def _spec(idx):
            kind_, i, j, t, half = stages[idx]
            nk = {"gmlp": 0, "attn": 0, "ffn": 2, "ple": 4}[kind_]
            return i, nk, t

        def norm_A(idx):
            if idx >= len(stages) or idx in nstate:
                return
            i, kind, t = _spec(idx)
            tsl = slice(t * T, (t + 1) * T)
            bank = balloc()
            nstate[idx] = {"bank": bank, "B": False}
            for c in range(8):
                si = ring("sq", 4)
                act(sqr[:, si, :], h[:, c, tsl], AF.Square, reads=(B_h[c][t],), writes=(B_sq[si],))
                stat_accum(bank, T, sqr[:, si, :], B_sq[si], c == 0, c == 7)

        def norm_B(idx):
            if idx >= len(stages):
                return
            norm_A(idx)
            if nstate[idx]["B"]:
                return
            nstate[idx]["B"] = True
            i, kind, t = _spec(idx)
            tsl = slice(t * T, (t + 1) * T)
            bank = nstate[idx]["bank"]
            ti = ring("tmp", 5)
            rstd_from_stat(bank, T, 1.0 / D, 0, ti)
            for c in range(8):
                stt(aT[:, c, :], h[:, c, tsl], pcol(i, kind, c), ps[:, bank, :], ALU.mult, ALU.mult,
                    reads=(B_h[c][t], B_ps[bank], B_par), writes=(B_aT[c],))
            bfree(bank)

        def norm_pre(i, kind, t):
            norm_B(cur[0])

        def hookA():
            norm_A(cur[0] + 1)

        def hookB():
            norm_B(cur[0] + 1)

        def evac_m(bank, c, stat_bank, first, last):
            si = ring("sq", 4)
            act(sqr[:, si, :], ps[:, bank, :], AF.Square, reads=(B_ps[bank],), writes=(B_sq[si],))
            tcopy(m[:, c, :], ps[:, bank, :], reads=(B_ps[bank],), writes=(B_m[c],))
            stat_accum(stat_bank, T, sqr[:, si, :], B_sq[si], first, last)

        def post_norm_residual(i, kind, t, stat_bank):
            tsl = slice(t * T, (t + 1) * T)
            ti = ring("tmp", 5)
            rstd_from_stat(stat_bank, T, 1.0 / D, 0, ti)
            for c in range(8):
                tt(m[:, c, :], m[:, c, :], ps[:, stat_bank, :], ALU.mult,
                   reads=(B_m[c], B_ps[stat_bank]), writes=(B_m[c],))
                stt(h[:, c, tsl], m[:, c, :], pcol(i, kind, c), h[:, c, tsl], ALU.mult, ALU.add,
                    reads=(B_m[c], B_h[c][t], B_par), writes=(B_h[c][t],))
            bfree(stat_bank)

        def proj_fm(w2d, n_out_chunks, rhs_fn, rhs_bufs, consume, kchunks=8, mid_hook=None, mid_at=3):
            nsl = (n_out_chunks + 3) // 4
            for sl in range(nsl):
                wv, wb = wslab(w2d, 0, sl * 512, kc=kchunks, ncols=min(512, n_out_chunks * 128 - sl * 512))
                if sl == 0 and n_out_chunks >= 4 and len(rhs_bufs) == kchunks:
                    banks = [balloc() for _ in range(4)]
                    for kc in range(kchunks):
                        for cc in range(4):
                            mm_group(banks[cc], (0, T), [(wv[:, kc, cc * 128:(cc + 1) * 128], rhs_fn(kc))],
                                     reads=(wb, rhs_bufs[kc]), start=(kc == 0), stop=(kc == kchunks - 1))
                    for cc in range(4):
                        consume(banks[cc], cc)
                    if mid_hook is not None and sl == mid_at:
                        mid_hook()
                    continue
                for cc in range(min(4, n_out_chunks - sl * 4)):
                    oc = sl * 4 + cc
                    bank = balloc()
                    pairs = [(wv[:, kc, cc * 128:(cc + 1) * 128], rhs_fn(kc)) for kc in range(kchunks)]
                    mm_group(bank, (0, T), pairs, reads=(wb,) + tuple(rhs_bufs))
                    consume(bank, oc)
                if mid_hook is not None and sl == mid_at:
                    mid_hook()

        def ffn(i, t):
            norm_pre(i, 2, t)
            w_up = f_w_up[i]
            w_dn = f_w_down[i]

            def cons_up(bank, oc):
                bg_step()
                ti = ring("tmp", 5)
                act(tmp[:, ti, :], ps[:, bank, :], AF.Relu, reads=(B_ps[bank],), writes=(B_tmp[ti],))
                bfree(bank)
                tt(hid[:, oc, :], tmp[:, ti, :], tmp[:, ti, :], ALU.mult, reads=(B_tmp[ti],), writes=(B_hid[oc],))
            proj_fm(w_up, 32, lambda kc: aT[:, kc, :], B_aT, cons_up, mid_hook=hookA, mid_at=3)
            hookB()
            stat_bank = None
            for nh in range(2):
                banks = [balloc() for _ in range(4)]
                for kb in range(4):
                    wv, wb = wslab(w_dn, kb * 1024, nh * 512)
                    for oc in range(4):
                        pairs = [(wv[:, kc, oc * 128:(oc + 1) * 128], hid[:, kb * 8 + kc, :]) for kc in range(8)]
                        mm_group(banks[oc], (0, T), pairs, reads=(wb,) + tuple(B_hid[kb * 8:kb * 8 + 8]),
                                 start=(kb == 0), stop=(kb == 3))
                if stat_bank is None:
                    stat_bank = balloc()
                for oc in range(4):
                    c = nh * 4 + oc
                    evac_m(banks[oc], c, stat_bank, c == 0, c == 7)
                    bfree(banks[oc])
            post_norm_residual(i, 3, t, stat_bank)

        def ple(i, t):
            tsl = slice(t * T, (t + 1) * T)
            norm_pre(i, 4, t)
            src = pT[i][:, tsl].rearrange("(kc p) n -> p kc n", p=128)
            P.emit("pool", lambda e: e.dma_start(out=pTb[:, :, :], in_=src), reads=(), writes=(B_pT,), sem=S_pT)
            wpv, wpb = wslab(ple_w_proj[i], 0, 0, kc=2, ncols=1024)
            stat_bank = balloc()
            wg = ple_w_gate[i]
            LAG1, LAG2 = 4, 6
            gslab = {}
            sqslot = {}
            for step in range(8 + LAG2):
                if step < 8:
                    oc = step
                    sl, cc = oc // 4, oc % 4
                    if cc == 0:
                        gslab[sl] = wslab(wg, 0, sl * 512)
                    wv, wb = gslab[sl]
                    bank = balloc()
                    pairs = [(wv[:, kc, cc * 128:(cc + 1) * 128], aT[:, kc, :]) for kc in range(8)]
                    mm_group(bank, (0, T), pairs, reads=(wb,) + tuple(B_aT))
                    act(m[:, oc, :], ps[:, bank, :], AF.Sigmoid, reads=(B_ps[bank], B_par), writes=(B_m[oc],),
                        bias=pcol(i, 6, oc))
                    bfree(bank)
                    if oc == 3:
                        hookA()
                k = step - LAG1
                if 0 <= k < 8:
                    bank2 = balloc()
                    pairs = [(wpv[:, kc, k * 128:(k + 1) * 128], pTb[:, kc, :]) for kc in range(2)]
                    mm_group(bank2, (0, T), pairs, reads=(wpb, B_pT))
                    tt(m[:, k, :], m[:, k, :], ps[:, bank2, :], ALU.mult,
                       reads=(B_m[k], B_ps[bank2]), writes=(B_m[k],))
                    bfree(bank2)
                    si = ring("sq", 4)
                    sqslot[k] = si
                    act(sqr[:, si, :], m[:, k, :], AF.Square, reads=(B_m[k],), writes=(B_sq[si],))
                k2 = step - LAG2
                if 0 <= k2 < 8:
                    si = sqslot[k2]
                    stat_accum(stat_bank, T, sqr[:, si, :], B_sq[si], k2 == 0, k2 == 7)
            hookB()
            post_norm_residual(i, 5, t, stat_bank)

        def gmlp_setup(j):
            wsf = tmp[:, 0:2, :].rearrange("p a n -> p (a n)")
            bsb = tmp[:, 2:4, :].rearrange("p a n -> p (a n)")
            P.emit("sp", lambda e: e.dma_start(out=wsf, in_=wsT_d[j]), reads=(), writes=(B_tmp[0], B_tmp[1]),
                   sem=S_misc[0])
            chk()
            bsrc = bass.AP(lnrow_d.tensor, j * 1024, [[0, 128], [1, 1024]])
            P.emit("sp", lambda e: e.dma_start(out=bsb, in_=bsrc), reads=(), writes=(B_tmp[2], B_tmp[3]),
                   sem=S_misc[1])
            chk()
            for g in range(8):
                tt(WcT[:, g, :], wsf[:, g * 128:(g + 1) * 128], cstf[:, 256:384], ALU.mult,
                   reads=(B_tmp[0], B_tmp[1], B_cstf), writes=(B_WcT,))
            gsetup2.append(lambda: gmlp_setup_p2(j, bsb))

        gsetup2 = []

        def gmlp_setup_p2(j, bsb):
            bank = balloc()
            bank2 = balloc()
            mm_group(bank, (0, T), [(ones_bf, WcT[:, 0:4, :].rearrange("p g t -> p (g t)"))], reads=(B_WcT, B_cb))
            mm_group(bank2, (0, T), [(ones_bf, WcT[:, 4:8, :].rearrange("p g t -> p (g t)"))], reads=(B_WcT, B_cb))
            chk()
            for g in range(8):
                bk = bank if g < 4 else bank2
                gg = g % 4
                k = 224 + j * 16 + 8 + g
                stt(E_v[:, g, :], ps[:, bk, gg * 128:(gg + 1) * 128], par[:, k:k + 1], bsb[:, g * 128:(g + 1) * 128],
                    ALU.mult, ALU.add, reads=(B_ps[bk], B_par, B_tmp[2], B_tmp[3]), writes=(B_E,))
            bfree(bank)
            bfree(bank2)

        def gmlp(i, j, t):
            norm_pre(i, 0, t)
            chk()
            w_in = a_w_in[j]

            wv0, wb0 = wslab(w_in, 0, 1024)
            wv1, wb1 = wslab(w_in, 0, 1536)
            for s_ in range(4):
                for hf, (wv, wb) in enumerate(((wv0, wb0), (wv1, wb1))):
                    bank = balloc()
                    pairs = [(aT[:, kc, s_ * 128:(s_ + 1) * 128], wv[:, kc, :]) for kc in range(8)]
                    mm_group(bank, (0, T), pairs, reads=(wb,) + tuple(B_aT))
                    act(vt_v[:, s_, hf * 512:(hf + 1) * 512], ps[:, bank, :], AF.Gelu_apprx_tanh,
                        reads=(B_ps[bank],), writes=(B_vt[s_],))
                    bfree(bank)
            hookA()
            for s_ in range(4):
                for hf in range(2):
                    P.emit("dve", lambda e, s_=s_, hf=hf: e.bn_stats(out=sm_[:, 8 + s_ * 12 + hf * 6: 8 + s_ * 12 + hf * 6 + 6],
                                                                    in_=vt_v[:, s_, hf * 512:(hf + 1) * 512]),
                           reads=(B_vt[s_],), writes=(B_small[8 + s_],))
                P.emit("dve", lambda e, s_=s_: e.bn_aggr(out=sm_[:, 56 + 2 * s_:58 + 2 * s_],
                                                        in_=sm_[:, 8 + s_ * 12: 8 + s_ * 12 + 12]),
                       reads=(B_small[8 + s_],), writes=(B_small[56 + s_],))
            chk()
            for s_ in range(4):
                act(sm_[:, 4 + s_:5 + s_], sm_[:, 57 + 2 * s_:58 + 2 * s_], AF.Ln,
                    reads=(B_small[56 + s_], B_small[1]), writes=(B_small[4 + s_],), bias=sm_[:, 1:2])
            for s_ in range(4):
                act(sm_[:, 4 + s_:5 + s_], sm_[:, 4 + s_:5 + s_], AF.Exp,
                    reads=(B_small[4 + s_],), writes=(B_small[4 + s_],), scale=-0.5)
            for s_ in range(4):
                ts(vh_v[:, s_, :], vt_v[:, s_, :], sm_[:, 56 + 2 * s_:57 + 2 * s_], sm_[:, 4 + s_:5 + s_],
                   ALU.subtract, ALU.mult, reads=(B_vt[s_], B_small[56 + s_], B_small[4 + s_]), writes=(B_vh[s_],))
            def cons_u(bank, oc):
                act(u_v[:, oc, :], ps[:, bank, :], AF.Gelu_apprx_tanh, reads=(B_ps[bank],), writes=(B_u[oc],))
                bfree(bank)
            proj_fm(w_in, 8, lambda kc: aT[:, kc, :], B_aT, cons_u)
            hookB()
            while gsetup2:
                gsetup2.pop(0)()
            for g in range(8):
                bank = balloc()
                for s_ in range(4):
                    mm_group(bank, (s_ * 128, (s_ + 1) * 128), [(vh_v[:, s_, g * 128:(g + 1) * 128], WcT[:, g, :])],
                             reads=(B_vh[s_], B_WcT))
                ti = ring("tmp", 5)
                k = 224 + j * 16 + g
                for s_ in range(4):
                    stt(tmp[:, ti, s_ * 128:(s_ + 1) * 128], ps[:, bank, s_ * 128:(s_ + 1) * 128], par[:, k:k + 1],
                        E_v[:, g, :], ALU.mult, ALU.add, reads=(B_ps[bank], B_par, B_E), writes=(B_tmp[ti],))
                bfree(bank)
                tt(u_v[:, g, :], u_v[:, g, :], tmp[:, ti, :], ALU.mult, reads=(B_u[g], B_tmp[ti]), writes=(B_u[g],))
            chk()
            stat_bank = balloc()

            def cons_o(bank, oc):
                evac_m(bank, oc, stat_bank, oc == 0, oc == 7)
                bfree(bank)
            proj_fm(a_w_out[j], 8, lambda kc: u_v[:, kc, :], B_u, cons_o)
            post_norm_residual(i, 1, t, stat_bank)

        def attn_setup(i, j):
            bg_drain()
            lambda_init = 0.8 - 0.6 * math.exp(-0.3 * i)
            lamb = tmp[:, 0, 0:256]
            lsrc = bass.AP(lam_d.tensor, j * 256, [[0, 128], [1, 256]])
            P.emit("sp", lambda e: e.dma_start(out=lamb, in_=lsrc), reads=(), writes=(B_tmp[0],), sem=S_misc[2])
            tt(tmp[:, 1, 0:64], lamb[:, 0:64], lamb[:, 64:128], ALU.mult, reads=(B_tmp[0],), writes=(B_tmp[1],))
            tt(tmp[:, 1, 64:128], lamb[:, 128:192], lamb[:, 192:256], ALU.mult, reads=(B_tmp[0],), writes=(B_tmp[1],))
            for q_ in range(2):
                P.emit("dve", lambda e, q_=q_: e.tensor_reduce(out=sm_[:, 2 + q_:3 + q_], in_=tmp[:, 1, q_ * 64:(q_ + 1) * 64],
                                                              axis=AX.X, op=ALU.add),
                       reads=(B_tmp[1],), writes=(B_small[2 + q_],))
                act(sm_[:, 2 + q_:3 + q_], sm_[:, 2 + q_:3 + q_], AF.Exp, reads=(B_small[2 + q_],), writes=(B_small[2 + q_],))
            tt(sm_[:, 2:3], sm_[:, 3:4], sm_[:, 2:3], ALU.subtract, reads=(B_small[2], B_small[3]), writes=(B_small[2],))
            ts(sm_[:, 2:3], sm_[:, 2:3], -lambda_init, None, ALU.add, None, reads=(B_small[2],), writes=(B_small[2],))
            ts(sm_[:, 3:4], par[:, 256 + j:257 + j], 1.0 - lambda_init, None, ALU.mult, None,
               reads=(B_par,), writes=(B_small[3],))

        def rot_p1(bank, t):
            tsl = slice(t * T, (t + 1) * T)
            i0 = ring("Pt", 8)
            i1 = ring("Pt", 8)
            tt(Pt[:, i0, :], ps[:, bank, :], CS[:, 0, tsl], ALU.mult, reads=(B_ps[bank], B_CS), writes=(B_Pt[i0],))
            tt(Pt[:, i1, :], ps[:, bank, :], CS[:, 1, tsl], ALU.mult, reads=(B_ps[bank], B_CS), writes=(B_Pt[i1],))
            bfree(bank)
            return i0, i1

        def rot_p2(i0, i1, dst, dst_bufs):
            b2 = balloc()
            mm_group(b2, (0, T), [(ident_bf, Pt[:, i0, :]), (pm_bf, Pt[:, i1, :])], reads=(B_Pt[i0], B_Pt[i1], B_cb))
            act(dst, ps[:, b2, :], AF.Copy, reads=(B_ps[b2],), writes=dst_bufs)
            bfree(b2)

        def attn_tile(i, j, half, t):
            tsl = slice(t * T, (t + 1) * T)
            wq = b_w_qkv[j]
            norm_pre(i, 0, t)
            wv, wb = wslab(wq, 0, half * 512)
            qrot = []
            for hc in range(4):
                bank = balloc()
                mm_group(bank, (0, T), [(wv[:, kc, hc * 128:(hc + 1) * 128], aT[:, kc, :]) for kc in range(8)],
                         reads=(wb,) + tuple(B_aT))
                qrot.append(rot_p1(bank, t))
            hookA()
            wv, wb = wslab(wq, 0, 1024 + half * 512)
            krot = []
            for hc in range(4):
                bank = balloc()
                mm_group(bank, (0, T), [(wv[:, kc, hc * 128:(hc + 1) * 128], aT[:, kc, :]) for kc in range(8)],
                         reads=(wb,) + tuple(B_aT))
                rot_p2(qrot[hc][0], qrot[hc][1], Qt[:, hc, :], (B_Qt[hc],))
                krot.append(rot_p1(bank, t))
            wv, wb = wslab(wq, 0, 2048 + half * 512)
            for s_ in range(4):
                bank = balloc()
                mm_group(bank, (0, T), [(aT[:, kc, s_ * 128:(s_ + 1) * 128], wv[:, kc, :]) for kc in range(8)],
                         reads=(wb,) + tuple(B_aT))
                rot_p2(krot[s_][0], krot[s_][1], Kc[:, s_, tsl], (B_K[s_][t],))
                act(Vc[:, t * 4 + s_, :], ps[:, bank, :], AF.Copy, reads=(B_ps[bank],), writes=(B_V[t * 4 + s_],))
                bfree(bank)
            hookB()
            nkt = 4 * t + 4
            fin_pending = []
            for hc in range(4):
                bO1, bO2, bS1, bS2 = balloc(), balloc(), balloc(), balloc()
                def emit_S(kt):
                    jd = kt - 4 * t
                    q0 = max(0, jd) * 128
                    kt_t = kt // 4
                    ksl = slice(kt * 128, (kt + 1) * 128)
                    b1, b2 = balloc(), balloc()
                    mm_group(b1, (q0, T), [(Kc[0:64, hc, ksl], Qt[0:64, hc, q0:T])],
                             reads=(B_K[hc][kt_t], B_Qt[hc]))
                    mm_group(b2, (q0, T), [(Kc[64:128, hc, ksl], Qt[64:128, hc, q0:T])],
                             reads=(B_K[hc][kt_t], B_Qt[hc]))
                    p1, p2 = ring("Pt", 8), ring("Pt", 8)
                    act(Pt[:, p1, q0:T], ps[:, b1, q0:T], AF.Exp, reads=(B_ps[b1],), writes=(B_Pt[p1],), scale=0.125)
                    act(Pt[:, p2, q0:T], ps[:, b2, q0:T], AF.Exp, reads=(B_ps[b2],), writes=(B_Pt[p2],), scale=0.125)
                    bfree(b1)
                    bfree(b2)
                    if jd >= 0:
                        for pp in (p1, p2):
                            tt(Pt[:, pp, q0:q0 + 128], Pt[:, pp, q0:q0 + 128], tri_bf, ALU.mult,
                               reads=(B_Pt[pp], B_cb), writes=(B_Pt[pp],))
                    return p1, p2, q0

                def emit_PV(kt, p1, p2, q0):
                    vsl = Vc[:, kt, hc * 128:(hc + 1) * 128]
                    first, last = kt == 0, kt == nkt - 1
                    mm_group(bO1, (q0, T), [(vsl, Pt[:, p1, q0:T])], reads=(B_V[kt], B_Pt[p1]), start=first, stop=last)
                    mm_group(bS1, (q0, T), [(ones_bf, Pt[:, p1, q0:T])], reads=(B_cb, B_Pt[p1]), start=first, stop=last)
                    mm_group(bO2, (q0, T), [(vsl, Pt[:, p2, q0:T])], reads=(B_V[kt], B_Pt[p2]), start=first, stop=last)
                    mm_group(bS2, (q0, T), [(ones_bf, Pt[:, p2, q0:T])], reads=(B_cb, B_Pt[p2]), start=first, stop=last)

                prev = emit_S(0)
                for kt in range(1, nkt):
                    nxt = emit_S(kt)
                    emit_PV(kt - 1, *prev)
                    prev = nxt
                    if kt == 1 and fin_pending:
                        fin_pending.pop(0)()
                emit_PV(nkt - 1, *prev)
                if fin_pending:
                    fin_pending.pop(0)()
                r1, r2, ta, tb = ring("tmp", 5), ring("tmp", 5), ring("tmp", 5), ring("tmp", 5)
                act(tmp[:, r1, :], ps[:, bS1, :], AF.Ln, reads=(B_ps[bS1],), writes=(B_tmp[r1],))
                tcopy(tmp[:, ta, :], ps[:, bO1, :], reads=(B_ps[bO1],), writes=(B_tmp[ta],))
                act(tmp[:, r2, :], ps[:, bS2, :], AF.Ln, reads=(B_ps[bS2],), writes=(B_tmp[r2],))
                tcopy(tmp[:, tb, :], ps[:, bO2, :], reads=(B_ps[bO2],), writes=(B_tmp[tb],))
                bfree(bS1)
                bfree(bO1)
                bfree(bS2)
                bfree(bO2)
                def fin2(hc=hc, r1=r1, r2=r2, ta=ta, tb=tb):
                    act(tmp[:, r1, :], tmp[:, r1, :], AF.Exp, reads=(B_tmp[r1],), writes=(B_tmp[r1],), scale=-1.0)
                    act(tmp[:, r2, :], tmp[:, r2, :], AF.Exp, reads=(B_tmp[r2],), writes=(B_tmp[r2],), scale=-1.0)
                    tt(tmp[:, ta, :], tmp[:, ta, :], tmp[:, r1, :], ALU.mult, reads=(B_tmp[ta], B_tmp[r1]), writes=(B_tmp[ta],))
                    tt(tmp[:, tb, :], tmp[:, tb, :], tmp[:, r2, :], ALU.mult, reads=(B_tmp[tb], B_tmp[r2]), writes=(B_tmp[tb],))
                    stt(tmp[:, ta, :], tmp[:, tb, :], sm_[:, 2:3], tmp[:, ta, :], ALU.mult, ALU.add,
                        reads=(B_tmp[ta], B_tmp[tb], B_small[2]), writes=(B_tmp[ta],))
                    si = ring("sq", 4)
                    act(sqr[:, si, :], tmp[:, ta, :], AF.Square, reads=(B_tmp[ta],), writes=(B_sq[si],))
                    sb_ = balloc()
                    stat_accum(sb_, T, sqr[:, si, :], B_sq[si], True, True)
                    rstd_from_stat(sb_, T, 1.0 / 128.0, 64, r1)
                    if half == 0:
                        dst, dbuf = Oall[:, hc, tsl], (B_Oall[hc][t],)
                    else:
                        dst, dbuf = Ot[:, hc, :], (B_Ot[hc],)
                    stt(dst, tmp[:, ta, :], sm_[:, 3:4], ps[:, sb_, :], ALU.mult, ALU.mult,
                        reads=(B_tmp[ta], B_small[3], B_ps[sb_]), writes=dbuf)
                    bfree(sb_)
                fin_pending.append(fin2)
            while fin_pending:
                fin_pending.pop(0)()
            if half == 1:
                stat_bank = balloc()

                def cons_o(bank, oc):
                    evac_m(bank, oc, stat_bank, oc == 0, oc == 7)
                    bfree(bank)

                def rhs(kc):
                    return Oall[:, kc, tsl] if kc < 4 else Ot[:, kc - 4, :]
                proj_fm(b_w_o[j], 8, rhs, [B_Oall[hh][t] for hh in range(4)] + B_Ot, cons_o)
                post_norm_residual(i, 1, t, stat_bank)

        def setup():
            P.emit("sp", lambda e: e.dma_start(out=par[:, :], in_=par_d[:, :]), reads=(), writes=(B_par,), sem=S_misc[0])
            P.emit("sp", lambda e: e.dma_start(out=cstf[:, :], in_=cst_d[:, :]), reads=(), writes=(B_cstf,), sem=S_misc[1])
            for c in range(8):
                P.emit("sp", lambda e, c=c: e.dma_start(out=h[:, c, :], in_=xT[c * 128:(c + 1) * 128, :]),
                       reads=(), writes=tuple(B_h[c]), sem=S_x[c])
            for k in range(3):
                tcopy(cb[:, k, :], cstf[:, k * 128:(k + 1) * 128], reads=(B_cstf,), writes=(B_cb,))
            P.emit("dve", lambda e: e.memset(cb[:, 3, :], 1.0), reads=(), writes=(B_cb,))
            P.emit("dve", lambda e: e.memset(sm_[:, 0:1], RMS_EPS), reads=(), writes=(B_small[0],))
            P.emit("dve", lambda e: e.memset(sm_[:, 1:2], LN_EPS), reads=(), writes=(B_small[1],))
            P.emit("dve", lambda e: e.memset(sm_[:, 64:65], SUBLN_EPS), reads=(), writes=(B_small[64],))
            if depth < 2:
                return
            sA = Qt[:, :, :].rearrange("p a n -> p (a n)").bitcast(F32)
            sB = Ot[:, :, :].rearrange("p a n -> p (a n)").bitcast(F32)
            sC = Pt[:, 0:4, :].rearrange("p a n -> p (a n)").bitcast(F32)
            sD = Pt[:, 4:8, :].rearrange("p a n -> p (a n)").bitcast(F32)
            bA, bB, bC, bD = tuple(B_Qt), tuple(B_Ot), tuple(B_Pt[0:4]), tuple(B_Pt[4:8])
            two_pi = 2.0 * math.pi
            c1 = 6.28125
            c2 = float(np.float32(two_pi - c1))
            c3 = float(two_pi - c1 - c2)
            HW = S // 2
            for hf in range(2):
                posi = sA.bitcast(I32)
                psrc = bass.AP(pos.tensor, hf * HW, [[0, 128], [1, HW]])
                bg.append(lambda posi=posi, psrc=psrc: P.emit(
                    "sp", lambda e: e.dma_start(out=posi, in_=psrc), reads=(), writes=bA, sem=S_misc[3]))
                bg.append(lambda posi=posi: tcopy(sB, posi, reads=bA, writes=bB))
                for which in range(2):
                    fcol = cstf[:, 384 + which:385 + which]
                    ki = sA.bitcast(I32)
                    csl = CS[:, which, hf * HW:(hf + 1) * HW]
                    if which == 0:
                        bg.append(lambda fcol=fcol: ts(sC, sB, fcol, math.pi / 2, ALU.mult, ALU.add,
                                                       reads=bB + (B_cstf,), writes=bC))
                    else:
                        bg.append(lambda fcol=fcol: ts(sC, sB, fcol, None, ALU.mult, None,
                                                       reads=bB + (B_cstf,), writes=bC))
                    bg.append(lambda: ts(sD, sC, 1.0 / two_pi, None, ALU.mult, None, reads=bC, writes=bD))
                    bg.append(lambda ki=ki: tcopy(ki, sD, reads=bD, writes=bA))
                    bg.append(lambda ki=ki: tcopy(sD, ki, reads=bA, writes=bD))
                    for cc in (c1, c2, c3):
                        bg.append(lambda cc=cc: stt(sC, sD, -cc, sC, ALU.mult, ALU.add, reads=bC + bD, writes=bC))
                    bg.append(lambda: ts(sD, sC, math.pi, -two_pi, ALU.is_gt, ALU.mult, reads=bC, writes=bD))
                    bg.append(lambda: tt(sC, sC, sD, ALU.add, reads=bC + bD, writes=bC))
                    bg.append(lambda: ts(sD, sC, -math.pi, two_pi, ALU.is_lt, ALU.mult, reads=bC, writes=bD))
                    bg.append(lambda: tt(sC, sC, sD, ALU.add, reads=bC + bD, writes=bC))
                    bg.append(lambda: ts(sC, sC, math.pi, -math.pi, ALU.min, ALU.max, reads=bC, writes=bC))
                    bg.append(lambda csl=csl: act(csl, sC, AF.Sin, reads=bC, writes=(B_CS,)))

        bg = []

        def bg_step():
            if bg:
                bg.pop(0)()

        def bg_drain():
            while bg:
                bg.pop(0)()

        for i in range(depth):
            j = i // 2
            if i % 2 == 0:
                for t in range(NT):
                    stages.append(("gmlp", i, j, t, 0))
            else:
                for half in range(2):
                    for t in range(NT):
                        stages.append(("attn", i, j, t, half))
            stages.append(("ffn", i, j, 0, 0))
            for t in range(1, NT):
                stages.append(("ffn", i, j, t, 0))
                stages.append(("ple", i, j, t - 1, 0))
            stages.append(("ple", i, j, NT - 1, 0))
        try:
            setup()
            chk()
            seen_layer = set()
            for idx, (kind_, i, j, t, half) in enumerate(stages):
                cur[0] = idx
                if kind_ in ("gmlp", "attn") and i not in seen_layer:
                    seen_layer.add(i)
                    if kind_ == "gmlp":
                        gmlp_setup(j)
                    else:
                        attn_setup(i, j)
                if kind_ == "gmlp":
                    gmlp(i, j, t)
                elif kind_ == "attn":
                    attn_tile(i, j, half, t)
                elif kind_ == "ffn":
                    ffn(i, t)
                else:
                    ple(i, t)
                chk()
        except _Stop:
            pass
        for c in range(8):
            P.emit("sp", lambda e, c=c: e.dma_start(out=outT[c * 128:(c + 1) * 128, :], in_=h[:, c, :]),
                   reads=tuple(B_h[c]), writes=(), sem=S_out[c])

        with nc.Block() as block:
            @block.tensor
            def _(e):
                P.replay(e, "pe")

            @block.scalar
            def _(e):
                P.replay(e, "act")

            @block.vector
            def _(e):
                P.replay(e, "dve")

            @block.gpsimd
            def _(e):
                P.replay(e, "pool")

            @block.sync
            def _(e):
                P.replay(e, "sp", final_waits=S_out)
    return nc


def _fm(vec):
    return np.ascontiguousarray(np.asarray(vec, np.float32).reshape(8, 128).T)


def _consts():
    c = np.zeros((128, NCST), np.float32)
    for m_ in range(128):
        d = m_ % 64
        if d < 8:
            c[m_ + 8, m_] = 1.0
        elif d < 16:
            c[m_ - 8, m_] = 1.0
    c[:, 128:256] = np.eye(128, dtype=np.float32)
    k = np.arange(128)[:, None]
    q = np.arange(128)[None, :]
    c[:, 256:384] = (q >= k).astype(np.float32)
    inv_freq = (500000.0 ** (-(np.arange(0, 16, 2, dtype=np.float32) / 16.0))).astype(np.float32)
    for p_ in range(128):
        d = p_ % 64
        if d < 16:
            f = inv_freq[d % 8]
            c[p_, 384] = f
            c[p_, 385] = f if d < 8 else -f
    return c


_NC_CACHE = {}


def _prep_inputs(inp, depth=DEPTH):
    par = np.zeros((128, NPAR), np.float32)
    kinds = ["g_mix_pre", "g_mix_post", "g_ffn_pre", "g_ffn_post", "g_ple_pre", "g_ple_post", "ple_b_gate"]
    for i in range(DEPTH):
        for k, nm in enumerate(kinds):
            par[:, (i * 7 + k) * 8:(i * 7 + k) * 8 + 8] = _fm(inp[nm][i])
    for j in range(2):
        par[:, 224 + j * 16:224 + j * 16 + 8] = _fm(inp["a_ln_g"][j])
        par[:, 224 + j * 16 + 8:224 + j * 16 + 16] = _fm(inp["a_ln_b"][j])
        par[:, 256 + j] = np.asarray(inp["b_subln_g"][j], np.float32)
    bsrow = np.ascontiguousarray(np.asarray(inp["a_b_s"], np.float32).reshape(2, 1024))
    wsT = np.ascontiguousarray(np.asarray(inp["a_w_s"], np.float32).transpose(0, 3, 1, 2).reshape(2, 128, 1024))
    lam = np.ascontiguousarray(np.stack([np.asarray(inp[k], np.float32) for k in
                                         ("b_lam_q1", "b_lam_k1", "b_lam_q2", "b_lam_k2")], axis=1).reshape(2, 256))
    shared = {
        "par": par, "cst": _consts(), "bsrow": bsrow, "wsT": wsT, "lam": lam,
    }
    for nm in ("a_w_in", "a_w_out", "b_w_qkv", "b_w_o", "f_w_up", "f_w_down", "ple_w_proj", "ple_w_gate"):
        shared[nm] = np.ascontiguousarray(np.asarray(inp[nm], np.float32))
    x = np.asarray(inp["x"], np.float32)
    p = np.asarray(inp["p"], np.float32)
    posn = np.asarray(inp["positions"], np.int32)
    maps = []
    for b in range(NCORES):
        mp = dict(shared)
        mp["xT"] = np.ascontiguousarray(x[b].T)
        mp["pT"] = np.ascontiguousarray(p[:, b].transpose(0, 2, 1))
        mp["pos"] = np.ascontiguousarray(posn[b].reshape(1, S))
        maps.append(mp)
    return maps


def kernel(**inputs):
    if "nc" not in _NC_CACHE:
        _NC_CACHE["nc"] = build_program(DEPTH)
    nc = _NC_CACHE["nc"]
    maps = _prep_inputs(inputs)
    res = run_bass_kernel_spmd(nc, maps, core_ids=list(range(NCORES)))
    out = np.stack([np.ascontiguousarray(res.results[b]["outT"].T) for b in range(NCORES)], axis=0)
    return out.astype(np.float32)
```

```python
import math
from contextlib import ExitStack

import numpy as np
import concourse.bass as bass
import concourse.mybir as mybir
from concourse.bass_utils import run_bass_kernel_spmd

F32 = mybir.dt.float32
BF16 = mybir.dt.bfloat16
I32 = mybir.dt.int32
AF = mybir.ActivationFunctionType
ALU = mybir.AluOpType
AX = mybir.AxisListType

D = 1024
S = 2048
T = 512
NT = S // T
DFF = 4096
PLE = 256
DEPTH = 4
NCORES = 8
RMS_EPS = 1e-6
LN_EPS = 1e-5
SUBLN_EPS = 1e-5
NPAR = 258
NCST = 386


class Sem:
    def __init__(self, h, step):
        self.h = h
        self.val = 0
        self.step = step


class Buf:
    def __init__(self, name, arena, lo, hi):
        self.name = name
        self.arena = arena
        self.lo = lo
        self.hi = hi
        self.last_w = None
        self.readers = {}
        self.excl = False


class Prog:
    ENGS = ("pe", "act", "dve", "pool", "sp")

    def __init__(self, nc, stack):
        self.nc = nc
        self.stack = stack
        self.q = {e: [] for e in self.ENGS}
        self.waited = {e: {} for e in self.ENGS}
        self.arenas = {}
        self.nsem = 0
        self.esem = {e: self.sem(1) for e in ("pe", "act", "dve", "pool")}
        self.nb = 0

    def sem(self, step=16):
        self.nsem += 1
        h = self.stack.enter_context(self.nc.semaphore("s%d" % self.nsem))
        return Sem(h, step)

    def buf(self, name, arena=None, lo=0, hi=1):
        if arena is None:
            self.nb += 1
            arena = "_u%d" % self.nb
        b = Buf(name, arena, lo, hi)
        self.arenas.setdefault(arena, []).append(b)
        return b

    def _conf(self, b):
        return [c for c in self.arenas[b.arena] if c.lo < b.hi and b.lo < c.hi]

    def emit(self, eng, fn, reads=(), writes=(), sem=None):
        is_dma = sem is not None
        s = sem if is_dma else self.esem[eng]
        own = self.esem.get(eng)
        need = {}

        def add(dep, kind):
            sm, v = dep
            if (not is_dma) and sm is own:
                if eng == "pe":
                    return
            if need.get(sm, 0) < v:
                need[sm] = v

        excl = [b for b in reads if b.excl]
        if excl:
            reads = tuple(b for b in reads if not b.excl)
            writes = tuple(writes) + tuple(excl)
        for b in reads:
            for c in self._conf(b):
                if c.last_w is not None:
                    add(c.last_w, "raw")
        for b in writes:
            for c in self._conf(b):
                if c.last_w is not None:
                    add(c.last_w, "waw")
                for sm, v in c.readers.items():
                    add((sm, v), "war")
        if is_dma and s.val > 0:
            need[s] = max(need.get(s, 0), s.val)
        waits = []
        wd = self.waited[eng]
        for sm, v in need.items():
            if wd.get(sm, 0) >= v:
                continue
            wd[sm] = v
            waits.append((sm, v))
        s.val += s.step
        self.q[eng].append((waits, fn, s))
        for b in reads:
            b.readers[s] = s.val
        for b in writes:
            b.last_w = (s, s.val)
            b.readers = {}

    def replay(self, e, eng, final_waits=()):
        for waits, fn, s in self.q[eng]:
            for sm, v in waits:
                e.wait_ge(sm.h, v)
            ins = fn(e)
            ins.then_inc(s.h, s.step)
        for sm in final_waits:
            e.wait_ge(sm.h, sm.val)


class _Stop(Exception):
    pass


def build_program(depth=DEPTH, stop=0):
    _stopc = [0]

    def chk():
        _stopc[0] += 1
        if stop and _stopc[0] >= stop:
            raise _Stop()

    nc = bass.Bass("TRN2", target_bir_lowering=False)
    dram = {}

    def din(name, shape, dt=F32):
        dram[name] = nc.dram_tensor(name, list(shape), dt, kind="ExternalInput").ap()
        return dram[name]

    xT = din("xT", [D, S])
    pT = din("pT", [DEPTH, PLE, S])
    pos = din("pos", [1, S], I32)
    par_d = din("par", [128, NPAR])
    cst_d = din("cst", [128, NCST])
    lnrow_d = din("bsrow", [2, 1024])
    wsT_d = din("wsT", [2, 128, 1024])
    lam_d = din("lam", [2, 256])
    a_w_in = din("a_w_in", [2, D, 2 * D])
    a_w_out = din("a_w_out", [2, D, D])
    b_w_qkv = din("b_w_qkv", [2, D, 3 * D])
    b_w_o = din("b_w_o", [2, D, D])
    f_w_up = din("f_w_up", [DEPTH, D, DFF])
    f_w_down = din("f_w_down", [DEPTH, DFF, D])
    ple_w_proj = din("ple_w_proj", [DEPTH, PLE, D])
    ple_w_gate = din("ple_w_gate", [DEPTH, D, D])
    outT = nc.dram_tensor("outT", [D, S], F32, kind="ExternalOutput").ap()

    with ExitStack() as st:
        def sb(name, shape, dt):
            return st.enter_context(nc.sbuf_tensor("sb_" + name, list(shape), dt))

        P = Prog(nc, st)
        h = sb("h", [128, 8, S], F32)
        OV = sb("OV", [128, 16384], BF16)
        AXA = sb("AXA", [128, 8192], BF16)
        aT = sb("aT", [128, 8, T], BF16)
        Qt = sb("Qt", [128, 4, T], BF16)
        Ot = sb("Ot", [128, 4, T], BF16)
        m = sb("m", [128, 8, T], F32)
        sqr = sb("sqr", [128, 4, T], BF16)
        CS = sb("CS", [128, 2, S], BF16)
        Pt = sb("Pt", [128, 8, T], BF16)
        tmp = sb("tmp", [128, 5, T], F32)
        Wr = sb("Wr", [128, 3, 4096], BF16)
        pTb = sb("pTb", [128, 2, T], BF16)
        par = sb("par", [128, NPAR], F32)
        cstf = sb("cstf", [128, NCST], F32)
        cb = sb("cb", [128, 4, 128], BF16)
        sm_ = sb("small", [128, 80], F32)
        ps = st.enter_context(nc.psum_tensor("ps", [128, 8, T], F32))

        OVf = OV[:, 8192:16384].bitcast(F32)
        hid = OV[:, :].rearrange("p (c n) -> p c n", n=T)
        u_v = OV[:, 0:4096].rearrange("p (c n) -> p c n", n=T)
        vh_v = OV[:, 4096:8192].rearrange("p (s n) -> p s n", n=1024)
        vt_v = OVf.rearrange("p (s n) -> p s n", n=1024)
        Kc = OV[:, 0:8192].rearrange("p (h n) -> p h n", n=S)
        Vc = OV[:, 8192:16384].rearrange("p (s n) -> p s n", n=512)
        Oall = AXA[:, :].rearrange("p (h n) -> p h n", n=S)
        E_v = AXA[:, 0:2048].bitcast(F32).rearrange("p (g t) -> p g t", t=128)
        WcT = AXA[:, 2048:3072].rearrange("p (g t) -> p g t", t=128)

        B_h = [[P.buf("h%d_%d" % (c, t)) for t in range(NT)] for c in range(8)]
        B_hid = [P.buf("hid%d" % c, "OV", c * 1024, (c + 1) * 1024) for c in range(32)]
        B_u = [P.buf("u%d" % c, "OV", c * 1024, (c + 1) * 1024) for c in range(8)]
        B_vh = [P.buf("vh%d" % s_, "OV", 8192 + s_ * 2048, 8192 + (s_ + 1) * 2048) for s_ in range(4)]
        B_vt = [P.buf("vt%d" % s_, "OV", 16384 + s_ * 4096, 16384 + (s_ + 1) * 4096) for s_ in range(4)]
        B_K = [[P.buf("K%d_%d" % (hh, t), "OV", hh * 4096 + t * 1024, hh * 4096 + (t + 1) * 1024)
                for t in range(NT)] for hh in range(4)]
        B_V = [P.buf("V%d" % s_, "OV", 16384 + s_ * 1024, 16384 + (s_ + 1) * 1024) for s_ in range(16)]
        B_OV_all = P.buf("OVall", "OV", 0, 32768)
        B_Oall = [[P.buf("Oa%d_%d" % (hh, t), "AX", hh * 4096 + t * 1024, hh * 4096 + (t + 1) * 1024)
                   for t in range(NT)] for hh in range(4)]
        B_E = P.buf("E", "AX", 0, 4096)
        B_WcT = P.buf("WcT", "AX", 4096, 6144)
        B_aT = [P.buf("aT%d" % c) for c in range(8)]
        B_Qt = [P.buf("Qt%d" % c) for c in range(4)]
        B_Ot = [P.buf("Ot%d" % c) for c in range(4)]
        B_m = [P.buf("m%d" % c) for c in range(8)]
        B_sq = [P.buf("sq%d" % c) for c in range(4)]
        B_CS = P.buf("CS")
        B_Pt = [P.buf("Pt%d" % c) for c in range(8)]
        B_tmp = [P.buf("tmp%d" % c) for c in range(5)]
        B_W = [P.buf("W%d" % c) for c in range(3)]
        S_W = [P.sem() for _ in range(3)]
        B_pT = P.buf("pTb")
        S_pT = P.sem()
        B_par = P.buf("par")
        B_cstf = P.buf("cstf")
        B_cb = P.buf("cb")
        B_small = [P.buf("small%d" % i) for i in range(80)]
        B_ps = [P.buf("ps%d" % i) for i in range(8)]
        for b_ in B_ps:
            b_.excl = True
        B_m_all = B_m
        S_misc = [P.sem() for _ in range(4)]
        S_x = [P.sem() for _ in range(8)]
        S_out = [P.sem() for _ in range(8)]

        free_banks = list(range(8))

        def balloc():
            return free_banks.pop(0)

        def bfree(b):
            free_banks.append(b)

        cnt = {"W": 0, "sq": 0, "Pt": 0, "tmp": 0}

        def ring(name, n):
            i = cnt[name] % n
            cnt[name] += 1
            return i

        def pcol(i, kind, c):
            k = (i * 7 + kind) * 8 + c
            return par[:, k:k + 1]

        def wload(src_ap, view_shape=None):
            i = ring("W", 3)
            a, b_ = src_ap.shape[1], src_ap.shape[2]
            dst = Wr[:, i, 0:a * b_].rearrange("p (a b) -> p a b", b=b_)
            P.emit("pool", lambda e, d=dst, s_=src_ap: e.dma_start(out=d, in_=s_),
                   reads=(), writes=(B_W[i],), sem=S_W[i])
            return dst, B_W[i]

        def wslab(w2d, k0, n0, kc=8, ncols=512):
            src = w2d[k0:k0 + kc * 128, n0:n0 + ncols].rearrange("(kc p) n -> p kc n", p=128)
            return wload(src)

        ones_bf = cb[:, 3, :]
        ident_bf = cb[:, 1, :]
        pm_bf = cb[:, 0, :]
        tri_bf = cb[:, 2, :]

        def mm_group(bank, cols, pairs, reads, start=True, stop=True):
            out = ps[:, bank, cols[0]:cols[1]]
            n = len(pairs)

            def fn(e):
                ins = None
                for i, (l, r) in enumerate(pairs):
                    ins = e.matmul(out, l, r, start=(start and i == 0), stop=(stop and i == n - 1))
                return ins
            P.emit("pe", fn, reads=reads, writes=(B_ps[bank],))

        def act(out, in_, func, reads, writes, scale=None, bias=None):
            kw = {}
            if scale is not None:
                kw["scale"] = scale
            if bias is not None:
                kw["bias"] = bias
            P.emit("act", lambda e: e.activation(out=out, in_=in_, func=func, **kw), reads=reads, writes=writes)

        def tt(out, in0, in1, op, reads, writes, eng="dve"):
            P.emit(eng, lambda e: e.tensor_tensor(out=out, in0=in0, in1=in1, op=op), reads=reads, writes=writes)

        def stt(out, in0, scalar, in1, op0, op1, reads, writes):
            P.emit("dve", lambda e: e.scalar_tensor_tensor(out=out, in0=in0, scalar=scalar, in1=in1, op0=op0, op1=op1),
                   reads=reads, writes=writes)

        def ts(out, in0, s1, s2, op0, op1, reads, writes, eng="dve"):
            if op1 is None:
                P.emit(eng, lambda e: e.tensor_scalar(out=out, in0=in0, scalar1=s1, scalar2=None, op0=op0),
                       reads=reads, writes=writes)
            else:
                P.emit(eng, lambda e: e.tensor_scalar(out=out, in0=in0, scalar1=s1, scalar2=s2, op0=op0, op1=op1),
                       reads=reads, writes=writes)

        def tcopy(out, in_, reads, writes, eng="dve"):
            P.emit(eng, lambda e: e.tensor_copy(out=out, in_=in_), reads=reads, writes=writes)

        eps_cols = {}

        def rstd_from_stat(bank, n, scale, eps_col, ti):
            t_ = tmp[:, ti, 0:n]
            act(t_, ps[:, bank, 0:n], AF.Ln, reads=(B_ps[bank], B_small[eps_col]), writes=(B_tmp[ti],),
                scale=scale, bias=sm_[:, eps_col:eps_col + 1])
            act(ps[:, bank, 0:n], t_, AF.Exp, reads=(B_tmp[ti],), writes=(B_ps[bank],), scale=-0.5)

        def stat_accum(bank, n, src_sq, src_buf, first, last):
            mm_group(bank, (0, n), [(ones_bf, src_sq)], reads=(src_buf, B_cb), start=first, stop=last)

        stages = []
        cur = [0]
        nstate = {}

        def _spec(idx):
            kind_, i, j, t, half = stages[idx]
            nk = {"gmlp": 0, "attn": 0, "ffn": 2, "ple": 4}[kind_]
            return i, nk, t

        def norm_A(idx):
            if idx >= len(stages) or idx in nstate:
                return
            i, kind, t = _spec(idx)
            tsl = slice(t * T, (t + 1) * T)
            bank = balloc()
            nstate[idx] = {"bank": bank, "B": False}
            for c in range(8):
                si = ring("sq", 4)
                act(sqr[:, si, :], h[:, c, tsl], AF.Square, reads=(B_h[c][t],), writes=(B_sq[si],))
                stat_accum(bank, T, sqr[:, si, :], B_sq[si], c == 0, c == 7)

        def norm_B(idx):
            if idx >= len(stages):
                return
            norm_A(idx)
            if nstate[idx]["B"]:
                return
            nstate[idx]["B"] = True
            i, kind, t = _spec(idx)
            tsl = slice(t * T, (t + 1) * T)
            bank = nstate[idx]["bank"]
            ti = ring("tmp", 5)
            rstd_from_stat(bank, T, 1.0 / D, 0, ti)
            for c in range(8):
                stt(aT[:, c, :], h[:, c, tsl], pcol(i, kind, c), ps[:, bank, :], ALU.mult, ALU.mult,
                    reads=(B_h[c][t], B_ps[bank], B_par), writes=(B_aT[c],))
            bfree(bank)

        def norm_pre(i, kind, t):
            norm_B(cur[0])

        def hookA():
            norm_A(cur[0] + 1)

        def hookB():
            norm_B(cur[0] + 1)

        def evac_m(bank, c, stat_bank, first, last):
            si = ring("sq", 4)
            act(sqr[:, si, :], ps[:, bank, :], AF.Square, reads=(B_ps[bank],), writes=(B_sq[si],))
            tcopy(m[:, c, :], ps[:, bank, :], reads=(B_ps[bank],), writes=(B_m[c],))
            stat_accum(stat_bank, T, sqr[:, si, :], B_sq[si], first, last)

        def post_norm_residual(i, kind, t, stat_bank):
            tsl = slice(t * T, (t + 1) * T)
            ti = ring("tmp", 5)
            rstd_from_stat(stat_bank, T, 1.0 / D, 0, ti)
            for c in range(8):
                tt(m[:, c, :], m[:, c, :], ps[:, stat_bank, :], ALU.mult,
                   reads=(B_m[c], B_ps[stat_bank]), writes=(B_m[c],))
                stt(h[:, c, tsl], m[:, c, :], pcol(i, kind, c), h[:, c, tsl], ALU.mult, ALU.add,
                    reads=(B_m[c], B_h[c][t], B_par), writes=(B_h[c][t],))
            bfree(stat_bank)

        def proj_fm(w2d, n_out_chunks, rhs_fn, rhs_bufs, consume, kchunks=8, mid_hook=None, mid_at=3):
            nsl = (n_out_chunks + 3) // 4
            for sl in range(nsl):
                wv, wb = wslab(w2d, 0, sl * 512, kc=kchunks, ncols=min(512, n_out_chunks * 128 - sl * 512))
                if sl == 0 and n_out_chunks >= 4 and len(rhs_bufs) == kchunks:
                    banks = [balloc() for _ in range(4)]
                    for kc in range(kchunks):
                        for cc in range(4):
                            mm_group(banks[cc], (0, T), [(wv[:, kc, cc * 128:(cc + 1) * 128], rhs_fn(kc))],
                                     reads=(wb, rhs_bufs[kc]), start=(kc == 0), stop=(kc == kchunks - 1))
                    for cc in range(4):
                        consume(banks[cc], cc)
                    if mid_hook is not None and sl == mid_at:
                        mid_hook()
                    continue
                for cc in range(min(4, n_out_chunks - sl * 4)):
                    oc = sl * 4 + cc
                    bank = balloc()
                    pairs = [(wv[:, kc, cc * 128:(cc + 1) * 128], rhs_fn(kc)) for kc in range(kchunks)]
                    mm_group(bank, (0, T), pairs, reads=(wb,) + tuple(rhs_bufs))
                    consume(bank, oc)
                if mid_hook is not None and sl == mid_at:
                    mid_hook()

        def ffn(i, t):
            norm_pre(i, 2, t)
            w_up = f_w_up[i]
            w_dn = f_w_down[i]

            def cons_up(bank, oc):
                bg_step()
                ti = ring("tmp", 5)
                act(tmp[:, ti, :], ps[:, bank, :], AF.Relu, reads=(B_ps[bank],), writes=(B_tmp[ti],))
                bfree(bank)
                tt(hid[:, oc, :], tmp[:, ti, :], tmp[:, ti, :], ALU.mult, reads=(B_tmp[ti],), writes=(B_hid[oc],))
            proj_fm(w_up, 32, lambda kc: aT[:, kc, :], B_aT, cons_up, mid_hook=hookA, mid_at=3)
            hookB()
            stat_bank = None
            for nh in range(2):
                banks = [balloc() for _ in range(4)]
                for kb in range(4):
                    wv, wb = wslab(w_dn, kb * 1024, nh * 512)
                    for oc in range(4):
                        pairs = [(wv[:, kc, oc * 128:(oc + 1) * 128], hid[:, kb * 8 + kc, :]) for kc in range(8)]
                        mm_group(banks[oc], (0, T), pairs, reads=(wb,) + tuple(B_hid[kb * 8:kb * 8 + 8]),
                                 start=(kb == 0), stop=(kb == 3))
                if stat_bank is None:
                    stat_bank = balloc()
                for oc in range(4):
                    c = nh * 4 + oc
                    evac_m(banks[oc], c, stat_bank, c == 0, c == 7)
                    bfree(banks[oc])
            post_norm_residual(i, 3, t, stat_bank)

        def ple(i, t):
            tsl = slice(t * T, (t + 1) * T)
            norm_pre(i, 4, t)
            src = pT[i][:, tsl].rearrange("(kc p) n -> p kc n", p=128)
            P.emit("pool", lambda e: e.dma_start(out=pTb[:, :, :], in_=src), reads=(), writes=(B_pT,), sem=S_pT)
            wpv, wpb = wslab(ple_w_proj[i], 0, 0, kc=2, ncols=1024)
            stat_bank = balloc()
            wg = ple_w_gate[i]
            LAG1, LAG2 = 4, 6
            gslab = {}
            sqslot = {}
            for step in range(8 + LAG2):
                if step < 8:
                    oc = step
                    sl, cc = oc // 4, oc % 4
                    if cc == 0:
                        gslab[sl] = wslab(wg, 0, sl * 512)
                    wv, wb = gslab[sl]
                    bank = balloc()
                    pairs = [(wv[:, kc, cc * 128:(cc + 1) * 128], aT[:, kc, :]) for kc in range(8)]
                    mm_group(bank, (0, T), pairs, reads=(wb,) + tuple(B_aT))
                    act(m[:, oc, :], ps[:, bank, :], AF.Sigmoid, reads=(B_ps[bank], B_par), writes=(B_m[oc],),
                        bias=pcol(i, 6, oc))
                    bfree(bank)
                    if oc == 3:
                        hookA()
                k = step - LAG1
                if 0 <= k < 8:
                    bank2 = balloc()
                    pairs = [(wpv[:, kc, k * 128:(k + 1) * 128], pTb[:, kc, :]) for kc in range(2)]
                    mm_group(bank2, (0, T), pairs, reads=(wpb, B_pT))
                    tt(m[:, k, :], m[:, k, :], ps[:, bank2, :], ALU.mult,
                       reads=(B_m[k], B_ps[bank2]), writes=(B_m[k],))
                    bfree(bank2)
                    si = ring("sq", 4)
                    sqslot[k] = si
                    act(sqr[:, si, :], m[:, k, :], AF.Square, reads=(B_m[k],), writes=(B_sq[si],))
                k2 = step - LAG2
                if 0 <= k2 < 8:
                    si = sqslot[k2]
                    stat_accum(stat_bank, T, sqr[:, si, :], B_sq[si], k2 == 0, k2 == 7)
            hookB()
            post_norm_residual(i, 5, t, stat_bank)

        def gmlp_setup(j):
            wsf = tmp[:, 0:2, :].rearrange("p a n -> p (a n)")
            bsb = tmp[:, 2:4, :].rearrange("p a n -> p (a n)")
            P.emit("sp", lambda e: e.dma_start(out=wsf, in_=wsT_d[j]), reads=(), writes=(B_tmp[0], B_tmp[1]),
                   sem=S_misc[0])
            chk()
            bsrc = bass.AP(lnrow_d.tensor, j * 1024, [[0, 128], [1, 1024]])
            P.emit("sp", lambda e: e.dma_start(out=bsb, in_=bsrc), reads=(), writes=(B_tmp[2], B_tmp[3]),
                   sem=S_misc[1])
            chk()
            for g in range(8):
                tt(WcT[:, g, :], wsf[:, g * 128:(g + 1) * 128], cstf[:, 256:384], ALU.mult,
                   reads=(B_tmp[0], B_tmp[1], B_cstf), writes=(B_WcT,))
            gsetup2.append(lambda: gmlp_setup_p2(j, bsb))

        gsetup2 = []

        def gmlp_setup_p2(j, bsb):
            bank = balloc()
            bank2 = balloc()
            mm_group(bank, (0, T), [(ones_bf, WcT[:, 0:4, :].rearrange("p g t -> p (g t)"))], reads=(B_WcT, B_cb))
            mm_group(bank2, (0, T), [(ones_bf, WcT[:, 4:8, :].rearrange("p g t -> p (g t)"))], reads=(B_WcT, B_cb))
            chk()
            for g in range(8):
                bk = bank if g < 4 else bank2
                gg = g % 4
                k = 224 + j * 16 + 8 + g
                stt(E_v[:, g, :], ps[:, bk, gg * 128:(gg + 1) * 128], par[:, k:k + 1], bsb[:, g * 128:(g + 1) * 128],
                    ALU.mult, ALU.add, reads=(B_ps[bk], B_par, B_tmp[2], B_tmp[3]), writes=(B_E,))
            bfree(bank)
            bfree(bank2)

        def gmlp(i, j, t):
            norm_pre(i, 0, t)
            chk()
            w_in = a_w_in[j]

            wv0, wb0 = wslab(w_in, 0, 1024)
            wv1, wb1 = wslab(w_in, 0, 1536)
            for s_ in range(4):
                for hf, (wv, wb) in enumerate(((wv0, wb0), (wv1, wb1))):
                    bank = balloc()
                    pairs = [(aT[:, kc, s_ * 128:(s_ + 1) * 128], wv[:, kc, :]) for kc in range(8)]
                    mm_group(bank, (0, T), pairs, reads=(wb,) + tuple(B_aT))
                    act(vt_v[:, s_, hf * 512:(hf + 1) * 512], ps[:, bank, :], AF.Gelu_apprx_tanh,
                        reads=(B_ps[bank],), writes=(B_vt[s_],))
                    bfree(bank)
            hookA()
            for s_ in range(4):
                for hf in range(2):
                    P.emit("dve", lambda e, s_=s_, hf=hf: e.bn_stats(out=sm_[:, 8 + s_ * 12 + hf * 6: 8 + s_ * 12 + hf * 6 + 6],
                                                                    in_=vt_v[:, s_, hf * 512:(hf + 1) * 512]),
                           reads=(B_vt[s_],), writes=(B_small[8 + s_],))
                P.emit("dve", lambda e, s_=s_: e.bn_aggr(out=sm_[:, 56 + 2 * s_:58 + 2 * s_],
                                                        in_=sm_[:, 8 + s_ * 12: 8 + s_ * 12 + 12]),
                       reads=(B_small[8 + s_],), writes=(B_small[56 + s_],))
            chk()
            for s_ in range(4):
                act(sm_[:, 4 + s_:5 + s_], sm_[:, 57 + 2 * s_:58 + 2 * s_], AF.Ln,
                    reads=(B_small[56 + s_], B_small[1]), writes=(B_small[4 + s_],), bias=sm_[:, 1:2])
            for s_ in range(4):
                act(sm_[:, 4 + s_:5 + s_], sm_[:, 4 + s_:5 + s_], AF.Exp,
                    reads=(B_small[4 + s_],), writes=(B_small[4 + s_],), scale=-0.5)
            for s_ in range(4):
                ts(vh_v[:, s_, :], vt_v[:, s_, :], sm_[:, 56 + 2 * s_:57 + 2 * s_], sm_[:, 4 + s_:5 + s_],
                   ALU.subtract, ALU.mult, reads=(B_vt[s_], B_small[56 + s_], B_small[4 + s_]), writes=(B_vh[s_],))
            def cons_u(bank, oc):
                act(u_v[:, oc, :], ps[:, bank, :], AF.Gelu_apprx_tanh, reads=(B_ps[bank],), writes=(B_u[oc],))
                bfree(bank)
            proj_fm(w_in, 8, lambda kc: aT[:, kc, :], B_aT, cons_u)
            while gsetup2:
                gsetup2.pop(0)()
            for g in range(8):
                bank = balloc()
                for s_ in range(4):
                    mm_group(bank, (s_ * 128, (s_ + 1) * 128), [(vh_v[:, s_, g * 128:(g + 1) * 128], WcT[:, g, :])],
                             reads=(B_vh[s_], B_WcT))
                ti = ring("tmp", 5)
                k = 224 + j * 16 + g
                for s_ in range(4):
                    stt(tmp[:, ti, s_ * 128:(s_ + 1) * 128], ps[:, bank, s_ * 128:(s_ + 1) * 128], par[:, k:k + 1],
                        E_v[:, g, :], ALU.mult, ALU.add, reads=(B_ps[bank], B_par, B_E), writes=(B_tmp[ti],))
                bfree(bank)
                tt(u_v[:, g, :], u_v[:, g, :], tmp[:, ti, :], ALU.mult, reads=(B_u[g], B_tmp[ti]), writes=(B_u[g],))
            hookB()
            stat_bank = balloc()

            def cons_o(bank, oc):
                evac_m(bank, oc, stat_bank, oc == 0, oc == 7)
                bfree(bank)
            proj_fm(a_w_out[j], 8, lambda kc: u_v[:, kc, :], B_u, cons_o)
            post_norm_residual(i, 1, t, stat_bank)

        def attn_setup(i, j):
            bg_drain()
            lambda_init = 0.8 - 0.6 * math.exp(-0.3 * i)
            lamb = tmp[:, 0, 0:256]
            lsrc = bass.AP(lam_d.tensor, j * 256, [[0, 128], [1, 256]])
            P.emit("sp", lambda e: e.dma_start(out=lamb, in_=lsrc), reads=(), writes=(B_tmp[0],), sem=S_misc[2])
            tt(tmp[:, 1, 0:64], lamb[:, 0:64], lamb[:, 64:128], ALU.mult, reads=(B_tmp[0],), writes=(B_tmp[1],))
            tt(tmp[:, 1, 64:128], lamb[:, 128:192], lamb[:, 192:256], ALU.mult, reads=(B_tmp[0],), writes=(B_tmp[1],))
            for q_ in range(2):
                P.emit("dve", lambda e, q_=q_: e.tensor_reduce(out=sm_[:, 2 + q_:3 + q_], in_=tmp[:, 1, q_ * 64:(q_ + 1) * 64],
                                                              axis=AX.X, op=ALU.add),
                       reads=(B_tmp[1],), writes=(B_small[2 + q_],))
                act(sm_[:, 2 + q_:3 + q_], sm_[:, 2 + q_:3 + q_], AF.Exp, reads=(B_small[2 + q_],), writes=(B_small[2 + q_],))
            tt(sm_[:, 2:3], sm_[:, 3:4], sm_[:, 2:3], ALU.subtract, reads=(B_small[2], B_small[3]), writes=(B_small[2],))
            ts(sm_[:, 2:3], sm_[:, 2:3], -lambda_init, None, ALU.add, None, reads=(B_small[2],), writes=(B_small[2],))
            ts(sm_[:, 3:4], par[:, 256 + j:257 + j], 1.0 - lambda_init, None, ALU.mult, None,
               reads=(B_par,), writes=(B_small[3],))

        def rot_p1(bank, t):
            tsl = slice(t * T, (t + 1) * T)
            i0 = ring("Pt", 8)
            i1 = ring("Pt", 8)
            tt(Pt[:, i0, :], ps[:, bank, :], CS[:, 0, tsl], ALU.mult, reads=(B_ps[bank], B_CS), writes=(B_Pt[i0],))
            tt(Pt[:, i1, :], ps[:, bank, :], CS[:, 1, tsl], ALU.mult, reads=(B_ps[bank], B_CS), writes=(B_Pt[i1],))
            bfree(bank)
            return i0, i1

        def rot_p2(i0, i1, dst, dst_bufs):
            b2 = balloc()
            mm_group(b2, (0, T), [(ident_bf, Pt[:, i0, :]), (pm_bf, Pt[:, i1, :])], reads=(B_Pt[i0], B_Pt[i1], B_cb))
            act(dst, ps[:, b2, :], AF.Copy, reads=(B_ps[b2],), writes=dst_bufs)
            bfree(b2)

        def attn_tile(i, j, half, t):
            tsl = slice(t * T, (t + 1) * T)
            wq = b_w_qkv[j]
            norm_pre(i, 0, t)
            wv, wb = wslab(wq, 0, half * 512)
            qrot = []
            for hc in range(4):
                bank = balloc()
                mm_group(bank, (0, T), [(wv[:, kc, hc * 128:(hc + 1) * 128], aT[:, kc, :]) for kc in range(8)],
                         reads=(wb,) + tuple(B_aT))
                qrot.append(rot_p1(bank, t))
            hookA()
            wv, wb = wslab(wq, 0, 1024 + half * 512)
            krot = []
            for hc in range(4):
                bank = balloc()
                mm_group(bank, (0, T), [(wv[:, kc, hc * 128:(hc + 1) * 128], aT[:, kc, :]) for kc in range(8)],
                         reads=(wb,) + tuple(B_aT))
                rot_p2(qrot[hc][0], qrot[hc][1], Qt[:, hc, :], (B_Qt[hc],))
                krot.append(rot_p1(bank, t))
            wv, wb = wslab(wq, 0, 2048 + half * 512)
            for s_ in range(4):
                bank = balloc()
                mm_group(bank, (0, T), [(aT[:, kc, s_ * 128:(s_ + 1) * 128], wv[:, kc, :]) for kc in range(8)],
                         reads=(wb,) + tuple(B_aT))
                rot_p2(krot[s_][0], krot[s_][1], Kc[:, s_, tsl], (B_K[s_][t],))
                act(Vc[:, t * 4 + s_, :], ps[:, bank, :], AF.Copy, reads=(B_ps[bank],), writes=(B_V[t * 4 + s_],))
                bfree(bank)
            hookB()
            nkt = 4 * t + 4
            fin_pending = []
            for hc in range(4):
                bO1, bO2, bS1, bS2 = balloc(), balloc(), balloc(), balloc()
                def emit_S(kt):
                    jd = kt - 4 * t
                    q0 = max(0, jd) * 128
                    kt_t = kt // 4
                    ksl = slice(kt * 128, (kt + 1) * 128)
                    b1, b2 = balloc(), balloc()
                    mm_group(b1, (q0, T), [(Kc[0:64, hc, ksl], Qt[0:64, hc, q0:T])],
                             reads=(B_K[hc][kt_t], B_Qt[hc]))
                    mm_group(b2, (q0, T), [(Kc[64:128, hc, ksl], Qt[64:128, hc, q0:T])],
                             reads=(B_K[hc][kt_t], B_Qt[hc]))
                    p1, p2 = ring("Pt", 8), ring("Pt", 8)
                    act(Pt[:, p1, q0:T], ps[:, b1, q0:T], AF.Exp, reads=(B_ps[b1],), writes=(B_Pt[p1],), scale=0.125)
                    act(Pt[:, p2, q0:T], ps[:, b2, q0:T], AF.Exp, reads=(B_ps[b2],), writes=(B_Pt[p2],), scale=0.125)
                    bfree(b1)
                    bfree(b2)
                    if jd >= 0:
                        for pp in (p1, p2):
                            tt(Pt[:, pp, q0:q0 + 128], Pt[:, pp, q0:q0 + 128], tri_bf, ALU.mult,
                               reads=(B_Pt[pp], B_cb), writes=(B_Pt[pp],))
                    return p1, p2, q0

                def emit_PV(kt, p1, p2, q0):
                    vsl = Vc[:, kt, hc * 128:(hc + 1) * 128]
                    first, last = kt == 0, kt == nkt - 1
                    mm_group(bO1, (q0, T), [(vsl, Pt[:, p1, q0:T])], reads=(B_V[kt], B_Pt[p1]), start=first, stop=last)
                    mm_group(bS1, (q0, T), [(ones_bf, Pt[:, p1, q0:T])], reads=(B_cb, B_Pt[p1]), start=first, stop=last)
                    mm_group(bO2, (q0, T), [(vsl, Pt[:, p2, q0:T])], reads=(B_V[kt], B_Pt[p2]), start=first, stop=last)
                    mm_group(bS2, (q0, T), [(ones_bf, Pt[:, p2, q0:T])], reads=(B_cb, B_Pt[p2]), start=first, stop=last)

                prev = emit_S(0)
                for kt in range(1, nkt):
                    nxt = emit_S(kt)
                    emit_PV(kt - 1, *prev)
                    prev = nxt
                    if kt == 1 and fin_pending:
                        fin_pending.pop(0)()
                emit_PV(nkt - 1, *prev)
                if fin_pending:
                    fin_pending.pop(0)()
                r1, r2, ta, tb = ring("tmp", 5), ring("tmp", 5), ring("tmp", 5), ring("tmp", 5)
                act(tmp[:, r1, :], ps[:, bS1, :], AF.Ln, reads=(B_ps[bS1],), writes=(B_tmp[r1],))
                tcopy(tmp[:, ta, :], ps[:, bO1, :], reads=(B_ps[bO1],), writes=(B_tmp[ta],))
                act(tmp[:, r2, :], ps[:, bS2, :], AF.Ln, reads=(B_ps[bS2],), writes=(B_tmp[r2],))
                tcopy(tmp[:, tb, :], ps[:, bO2, :], reads=(B_ps[bO2],), writes=(B_tmp[tb],))
                bfree(bS1)
                bfree(bO1)
                bfree(bS2)
                bfree(bO2)
                def fin2(hc=hc, r1=r1, r2=r2, ta=ta, tb=tb):
                    act(tmp[:, r1, :], tmp[:, r1, :], AF.Exp, reads=(B_tmp[r1],), writes=(B_tmp[r1],), scale=-1.0)
                    act(tmp[:, r2, :], tmp[:, r2, :], AF.Exp, reads=(B_tmp[r2],), writes=(B_tmp[r2],), scale=-1.0)
                    tt(tmp[:, ta, :], tmp[:, ta, :], tmp[:, r1, :], ALU.mult, reads=(B_tmp[ta], B_tmp[r1]), writes=(B_tmp[ta],))
                    tt(tmp[:, tb, :], tmp[:, tb, :], tmp[:, r2, :], ALU.mult, reads=(B_tmp[tb], B_tmp[r2]), writes=(B_tmp[tb],))
                    stt(tmp[:, ta, :], tmp[:, tb, :], sm_[:, 2:3], tmp[:, ta, :], ALU.mult, ALU.add,
                        reads=(B_tmp[ta], B_tmp[tb], B_small[2]), writes=(B_tmp[ta],))
                    si = ring("sq", 4)
                    act(sqr[:, si, :], tmp[:, ta, :], AF.Square, reads=(B_tmp[ta],), writes=(B_sq[si],))
                    sb_ = balloc()
                    stat_accum(sb_, T, sqr[:, si, :], B_sq[si], True, True)
                    rstd_from_stat(sb_, T, 1.0 / 128.0, 64, r1)
                    if half == 0:
                        dst, dbuf = Oall[:, hc, tsl], (B_Oall[hc][t],)
                    else:
                        dst, dbuf = Ot[:, hc, :], (B_Ot[hc],)
                    stt(dst, tmp[:, ta, :], sm_[:, 3:4], ps[:, sb_, :], ALU.mult, ALU.mult,
                        reads=(B_tmp[ta], B_small[3], B_ps[sb_]), writes=dbuf)
                    bfree(sb_)
                fin_pending.append(fin2)
            while fin_pending:
                fin_pending.pop(0)()
            if half == 1:
                stat_bank = balloc()

                def cons_o(bank, oc):
                    evac_m(bank, oc, stat_bank, oc == 0, oc == 7)
                    bfree(bank)

                def rhs(kc):
                    return Oall[:, kc, tsl] if kc < 4 else Ot[:, kc - 4, :]
                proj_fm(b_w_o[j], 8, rhs, [B_Oall[hh][t] for hh in range(4)] + B_Ot, cons_o)
                post_norm_residual(i, 1, t, stat_bank)

        def setup():
            P.emit("sp", lambda e: e.dma_start(out=par[:, :], in_=par_d[:, :]), reads=(), writes=(B_par,), sem=S_misc[0])
            P.emit("sp", lambda e: e.dma_start(out=cstf[:, :], in_=cst_d[:, :]), reads=(), writes=(B_cstf,), sem=S_misc[1])
            for c in range(8):
                P.emit("sp", lambda e, c=c: e.dma_start(out=h[:, c, :], in_=xT[c * 128:(c + 1) * 128, :]),
                       reads=(), writes=tuple(B_h[c]), sem=S_x[c])
            for k in range(3):
                tcopy(cb[:, k, :], cstf[:, k * 128:(k + 1) * 128], reads=(B_cstf,), writes=(B_cb,))
            P.emit("dve", lambda e: e.memset(cb[:, 3, :], 1.0), reads=(), writes=(B_cb,))
            P.emit("dve", lambda e: e.memset(sm_[:, 0:1], RMS_EPS), reads=(), writes=(B_small[0],))
            P.emit("dve", lambda e: e.memset(sm_[:, 1:2], LN_EPS), reads=(), writes=(B_small[1],))
            P.emit("dve", lambda e: e.memset(sm_[:, 64:65], SUBLN_EPS), reads=(), writes=(B_small[64],))
            if depth < 2:
                return
            sA = Qt[:, :, :].rearrange("p a n -> p (a n)").bitcast(F32)
            sB = Ot[:, :, :].rearrange("p a n -> p (a n)").bitcast(F32)
            sC = Pt[:, 0:4, :].rearrange("p a n -> p (a n)").bitcast(F32)
            sD = Pt[:, 4:8, :].rearrange("p a n -> p (a n)").bitcast(F32)
            bA, bB, bC, bD = tuple(B_Qt), tuple(B_Ot), tuple(B_Pt[0:4]), tuple(B_Pt[4:8])
            two_pi = 2.0 * math.pi
            c1 = 6.28125
            c2 = float(np.float32(two_pi - c1))
            c3 = float(two_pi - c1 - c2)
            HW = S // 2
            for hf in range(2):
                posi = sA.bitcast(I32)
                psrc = bass.AP(pos.tensor, hf * HW, [[0, 128], [1, HW]])
                bg.append(lambda posi=posi, psrc=psrc: P.emit(
                    "sp", lambda e: e.dma_start(out=posi, in_=psrc), reads=(), writes=bA, sem=S_misc[3]))
                bg.append(lambda posi=posi: tcopy(sB, posi, reads=bA, writes=bB))
                for which in range(2):
                    fcol = cstf[:, 384 + which:385 + which]
                    ki = sA.bitcast(I32)
                    csl = CS[:, which, hf * HW:(hf + 1) * HW]
                    if which == 0:
                        bg.append(lambda fcol=fcol: ts(sC, sB, fcol, math.pi / 2, ALU.mult, ALU.add,
                                                       reads=bB + (B_cstf,), writes=bC))
                    else:
                        bg.append(lambda fcol=fcol: ts(sC, sB, fcol, None, ALU.mult, None,
                                                       reads=bB + (B_cstf,), writes=bC))
                    bg.append(lambda: ts(sD, sC, 1.0 / two_pi, None, ALU.mult, None, reads=bC, writes=bD))
                    bg.append(lambda ki=ki: tcopy(ki, sD, reads=bD, writes=bA))
                    bg.append(lambda ki=ki: tcopy(sD, ki, reads=bA, writes=bD))
                    for cc in (c1, c2, c3):
                        bg.append(lambda cc=cc: stt(sC, sD, -cc, sC, ALU.mult, ALU.add, reads=bC + bD, writes=bC))
                    bg.append(lambda: ts(sD, sC, math.pi, -two_pi, ALU.is_gt, ALU.mult, reads=bC, writes=bD))
                    bg.append(lambda: tt(sC, sC, sD, ALU.add, reads=bC + bD, writes=bC))
                    bg.append(lambda: ts(sD, sC, -math.pi, two_pi, ALU.is_lt, ALU.mult, reads=bC, writes=bD))
                    bg.append(lambda: tt(sC, sC, sD, ALU.add, reads=bC + bD, writes=bC))
                    bg.append(lambda: ts(sC, sC, math.pi, -math.pi, ALU.min, ALU.max, reads=bC, writes=bC))
                    bg.append(lambda csl=csl: act(csl, sC, AF.Sin, reads=bC, writes=(B_CS,)))

        bg = []

        def bg_step():
            if bg:
                bg.pop(0)()

        def bg_drain():
            while bg:
                bg.pop(0)()

        for i in range(depth):
            j = i // 2
            if i % 2 == 0:
                for t in range(NT):
                    stages.append(("gmlp", i, j, t, 0))
            else:
                for half in range(2):
                    for t in range(NT):
                        stages.append(("attn", i, j, t, half))
            stages.append(("ffn", i, j, 0, 0))
            for t in range(1, NT):
                stages.append(("ffn", i, j, t, 0))
                stages.append(("ple", i, j, t - 1, 0))
            stages.append(("ple", i, j, NT - 1, 0))
        try:
            setup()
            chk()
            seen_layer = set()
            for idx, (kind_, i, j, t, half) in enumerate(stages):
                cur[0] = idx
                if kind_ in ("gmlp", "attn") and i not in seen_layer:
                    seen_layer.add(i)
                    if kind_ == "gmlp":
                        gmlp_setup(j)
                    else:
                        attn_setup(i, j)
                if kind_ == "gmlp":
                    gmlp(i, j, t)
                elif kind_ == "attn":
                    attn_tile(i, j, half, t)
                elif kind_ == "ffn":
                    ffn(i, t)
                else:
                    ple(i, t)
                chk()
        except _Stop:
            pass
        for c in range(8):
            P.emit("sp", lambda e, c=c: e.dma_start(out=outT[c * 128:(c + 1) * 128, :], in_=h[:, c, :]),
                   reads=tuple(B_h[c]), writes=(), sem=S_out[c])

        with nc.Block() as block:
            @block.tensor
            def _(e):
                P.replay(e, "pe")

            @block.scalar
            def _(e):
                P.replay(e, "act")

            @block.vector
            def _(e):
                P.replay(e, "dve")

            @block.gpsimd
            def _(e):
                P.replay(e, "pool")

            @block.sync
            def _(e):
                P.replay(e, "sp", final_waits=S_out)
    return nc


def _fm(vec):
    return np.ascontiguousarray(np.asarray(vec, np.float32).reshape(8, 128).T)


def _consts():
    c = np.zeros((128, NCST), np.float32)
    for m_ in range(128):
        d = m_ % 64
        if d < 8:
            c[m_ + 8, m_] = 1.0
        elif d < 16:
            c[m_ - 8, m_] = 1.0
    c[:, 128:256] = np.eye(128, dtype=np.float32)
    k = np.arange(128)[:, None]
    q = np.arange(128)[None, :]
    c[:, 256:384] = (q >= k).astype(np.float32)
    inv_freq = (500000.0 ** (-(np.arange(0, 16, 2, dtype=np.float32) / 16.0))).astype(np.float32)
    for p_ in range(128):
        d = p_ % 64
        if d < 16:
            f = inv_freq[d % 8]
            c[p_, 384] = f
            c[p_, 385] = f if d < 8 else -f
    return c


_NC_CACHE = {}


def _prep_inputs(inp, depth=DEPTH):
    par = np.zeros((128, NPAR), np.float32)
    kinds = ["g_mix_pre", "g_mix_post", "g_ffn_pre", "g_ffn_post", "g_ple_pre", "g_ple_post", "ple_b_gate"]
    for i in range(DEPTH):
        for k, nm in enumerate(kinds):
            par[:, (i * 7 + k) * 8:(i * 7 + k) * 8 + 8] = _fm(inp[nm][i])
    for j in range(2):
        par[:, 224 + j * 16:224 + j * 16 + 8] = _fm(inp["a_ln_g"][j])
        par[:, 224 + j * 16 + 8:224 + j * 16 + 16] = _fm(inp["a_ln_b"][j])
        par[:, 256 + j] = np.asarray(inp["b_subln_g"][j], np.float32)
    bsrow = np.ascontiguousarray(np.asarray(inp["a_b_s"], np.float32).reshape(2, 1024))
    wsT = np.ascontiguousarray(np.asarray(inp["a_w_s"], np.float32).transpose(0, 3, 1, 2).reshape(2, 128, 1024))
    lam = np.ascontiguousarray(np.stack([np.asarray(inp[k], np.float32) for k in
                                         ("b_lam_q1", "b_lam_k1", "b_lam_q2", "b_lam_k2")], axis=1).reshape(2, 256))
    shared = {
        "par": par, "cst": _consts(), "bsrow": bsrow, "wsT": wsT, "lam": lam,
    }
    for nm in ("a_w_in", "a_w_out", "b_w_qkv", "b_w_o", "f_w_up", "f_w_down", "ple_w_proj", "ple_w_gate"):
        shared[nm] = np.ascontiguousarray(np.asarray(inp[nm], np.float32))
    x = np.asarray(inp["x"], np.float32)
    p = np.asarray(inp["p"], np.float32)
    posn = np.asarray(inp["positions"], np.int32)
    maps = []
    for b in range(NCORES):
        mp = dict(shared)
        mp["xT"] = np.ascontiguousarray(x[b].T)
        mp["pT"] = np.ascontiguousarray(p[:, b].transpose(0, 2, 1))
        mp["pos"] = np.ascontiguousarray(posn[b].reshape(1, S))
        maps.append(mp)
    return maps


def kernel(**inputs):
    if "nc" not in _NC_CACHE:
        _NC_CACHE["nc"] = build_program(DEPTH)
    nc = _NC_CACHE["nc"]
    maps = _prep_inputs(inputs)
    res = run_bass_kernel_spmd(nc, maps, core_ids=list(range(NCORES)))
    out = np.stack([np.ascontiguousarray(res.results[b]["outT"].T) for b in range(NCORES)], axis=0)
    return out.astype(np.float32)
```
